# Optimizing a Trainium2 kernel written in Bass

```python
import jax, jax.numpy as jnp
from jax import lax
import numpy as np

D_MODEL = 2048
BATCH = 4
SEQ = 4096
DEPTH = 4

N_MIXERS = 2
HEAD_DIM = 128
N_HEADS = D_MODEL // HEAD_DIM
MOBA_BLOCK = 256
MOBA_TOPK = 3
MOBA_QCHUNK = 16
HGRN_EXPAND = 128
HGRN_HEADS = D_MODEL // HGRN_EXPAND
HGRN_CHUNK = 64
N_GROUPS = 4
EXPERTS_PER_GROUP = 8
N_EXPERTS = N_GROUPS * EXPERTS_PER_GROUP
TOP_E = 2
D_EXPERT = 512
MOE_BLOCK = 128
N_MOBA_LAYERS = (DEPTH + 1) // 2
N_HGRN_LAYERS = DEPTH // 2
DEEPNORM_ALPHA = (2 * DEPTH) ** 0.25
DEEPNORM_BETA = (8 * DEPTH) ** -0.25
LN_EPS = 1e-5
RMS_EPS = 1e-6

kernel_name = "hybrid_moba_hgrn2_hmoe_deepnorm"


def layer_norm(x, g, b):
    xf = x.astype(jnp.float32)
    mu = jnp.mean(xf, axis=-1, keepdims=True)
    var = jnp.mean(jnp.square(xf - mu), axis=-1, keepdims=True)
    return ((xf - mu) * lax.rsqrt(var + LN_EPS) * g.astype(jnp.float32) + b.astype(jnp.float32)).astype(x.dtype)


def moba_attention(x, w_in, w_o):
    B, S, D = x.shape
    q, k, v = jnp.split(x @ w_in, 3, axis=-1)
    heads = lambda t: t.reshape(B, S, N_HEADS, HEAD_DIM).transpose(0, 2, 1, 3)
    q = heads(q) * (HEAD_DIM ** -0.5)
    k, v = heads(k), heads(v)
    nb = -(-S // MOBA_BLOCK)
    pad = nb * MOBA_BLOCK - S
    kp = jnp.pad(k, ((0, 0), (0, 0), (0, pad), (0, 0))).reshape(B, N_HEADS, nb, MOBA_BLOCK, HEAD_DIM)
    vp = jnp.pad(v, ((0, 0), (0, 0), (0, pad), (0, 0))).reshape(B, N_HEADS, nb, MOBA_BLOCK, HEAD_DIM)
    k_mean = jnp.mean(kp.astype(jnp.float32), axis=3).astype(x.dtype)
    gate = jnp.einsum('bhsd,bhnd->bhsn', q, k_mean).astype(jnp.float32)
    q_blk = jnp.arange(S) // MOBA_BLOCK
    fully_past = jnp.arange(nb)[None, :] < q_blk[:, None]
    gate = jnp.where(fully_past, gate, -jnp.inf)
    topk = min(MOBA_TOPK, nb)
    gval, sel = lax.top_k(gate, topk)
    sel_valid = gval > -jnp.inf
    gather_blocks = jax.vmap(jax.vmap(lambda blocks, idx: blocks[idx]))
    n_sel = topk * MOBA_BLOCK

    def chunk(c):
        s0 = c * MOBA_QCHUNK
        qc = lax.dynamic_slice_in_dim(q, s0, MOBA_QCHUNK, axis=2)
        selc = lax.dynamic_slice_in_dim(sel, s0, MOBA_QCHUNK, axis=2)
        validc = lax.dynamic_slice_in_dim(sel_valid, s0, MOBA_QCHUNK, axis=2)
        kg = gather_blocks(kp, selc)
        vg = gather_blocks(vp, selc)
        own = s0 // MOBA_BLOCK
        k_own = lax.dynamic_index_in_dim(kp, own, axis=2, keepdims=False)
        v_own = lax.dynamic_index_in_dim(vp, own, axis=2, keepdims=False)
        s_sel = jnp.einsum('bhqd,bhqnkd->bhqnk', qc, kg).astype(jnp.float32)
        s_sel = jnp.where(validc[..., None], s_sel, -jnp.inf).reshape(B, N_HEADS, MOBA_QCHUNK, n_sel)
        s_own = jnp.einsum('bhqd,bhkd->bhqk', qc, k_own).astype(jnp.float32)
        qpos = s0 + jnp.arange(MOBA_QCHUNK)
        kpos = own * MOBA_BLOCK + jnp.arange(MOBA_BLOCK)
        s_own = jnp.where(kpos[None, :] <= qpos[:, None], s_own, -jnp.inf)
        p = jax.nn.softmax(jnp.concatenate([s_sel, s_own], axis=-1), axis=-1).astype(v.dtype)
        vg = vg.reshape(B, N_HEADS, MOBA_QCHUNK, n_sel, HEAD_DIM)
        return (jnp.einsum('bhqn,bhqnd->bhqd', p[..., :n_sel], vg)
                + jnp.einsum('bhqk,bhkd->bhqd', p[..., n_sel:], v_own))

    o = lax.map(chunk, jnp.arange(S // MOBA_QCHUNK))
    o = o.transpose(1, 2, 0, 3, 4).reshape(B, N_HEADS, S, HEAD_DIM)
    o = o.transpose(0, 2, 1, 3).reshape(B, S, D)
    return o @ w_o


def hgrn2(x, w_in, g_norm, lb, w_o):
    B, S, D = x.shape
    q, f, i, g = jnp.split(x @ w_in, 4, axis=-1)
    q = jax.nn.silu(q.astype(jnp.float32))
    lbf = lb.astype(jnp.float32)
    fg = lbf + (1.0 - lbf) * jax.nn.sigmoid(f.astype(jnp.float32))
    log_f = jnp.log(fg)
    k = 1.0 - fg
    n_chunks = S // HGRN_CHUNK
    to_chunks = lambda t: t.reshape(B, n_chunks, HGRN_CHUNK, HGRN_HEADS, HGRN_EXPAND).transpose(1, 0, 3, 2, 4)
    qc, kc, lc = to_chunks(q), to_chunks(k), to_chunks(log_f)
    ic = i.astype(jnp.float32).reshape(B, n_chunks, HGRN_CHUNK, HGRN_HEADS, HEAD_DIM).transpose(1, 0, 3, 2, 4)
    causal = (jnp.arange(HGRN_CHUNK)[:, None] >= jnp.arange(HGRN_CHUNK)[None, :])[..., None]

    def step(state, inp):
        qb, kb, lb_, ib = inp
        G = jnp.cumsum(lb_, axis=-2)
        o_inter = jnp.einsum('bhck,bhkv->bhcv', qb * jnp.exp(G), state)
        diff = G[:, :, :, None, :] - G[:, :, None, :, :]
        decay = jnp.where(causal, jnp.exp(jnp.where(causal, diff, 0.0)), 0.0)
        A = jnp.einsum('bhtk,bhsk,bhtsk->bhts', qb, kb, decay)
        o_intra = jnp.einsum('bhts,bhsv->bhtv', A, ib)
        G_last = G[:, :, -1:, :]
        state = (jnp.exp(G_last[:, :, 0, :])[..., None] * state
                 + jnp.einsum('bhsk,bhsv->bhkv', kb * jnp.exp(G_last - G), ib))
        return state, o_inter + o_intra

    s0 = jnp.zeros((B, HGRN_HEADS, HGRN_EXPAND, HEAD_DIM), jnp.float32)
    _, o = lax.scan(step, s0, (qc, kc, lc, ic))
    o = o.transpose(1, 0, 3, 2, 4).reshape(B, S, HGRN_HEADS, HEAD_DIM)
    o = o * lax.rsqrt(jnp.mean(jnp.square(o), axis=-1, keepdims=True) + RMS_EPS) * g_norm.astype(jnp.float32)
    o = o.reshape(B, S, D) * jax.nn.silu(g.astype(jnp.float32))
    return o.astype(x.dtype) @ w_o


def hierarchical_moe(x, w_rg, b_rg, w_re, b_re, w_gate, w_up, w_down):
    B, S, D = x.shape
    T = B * S
    xt = x.reshape(T, D)
    pg = jax.nn.softmax((xt @ w_rg).astype(jnp.float32) + b_rg.astype(jnp.float32), axis=-1)
    g_w, g_idx = lax.top_k(pg, 1)
    g_w, g_idx = g_w[:, 0], g_idx[:, 0]
    le_all = jnp.einsum('td,gde->tge', xt, w_re).astype(jnp.float32) + b_re.astype(jnp.float32)
    le = jnp.take_along_axis(le_all, g_idx[:, None, None], axis=1)[:, 0]
    e_w, e_idx = lax.top_k(jax.nn.softmax(le, axis=-1), TOP_E)
    e_w = e_w / jnp.sum(e_w, axis=-1, keepdims=True)
    weights = (g_w[:, None] * e_w).reshape(-1)
    eid = (g_idx[:, None] * EXPERTS_PER_GROUP + e_idx).reshape(-1).astype(jnp.int32)
    tok = jnp.repeat(jnp.arange(T, dtype=jnp.int32), TOP_E)
    n_assign = T * TOP_E
    order = jnp.argsort(eid)
    eid_s, tok_s, w_s = eid[order], tok[order], weights[order]
    counts = jnp.bincount(eid, length=N_EXPERTS)
    starts = jnp.cumsum(counts) - counts
    padded = (counts + MOE_BLOCK - 1) // MOE_BLOCK * MOE_BLOCK
    pends = jnp.cumsum(padded)
    pstarts = pends - padded
    dest = pstarts[eid_s] + (jnp.arange(n_assign) - starts[eid_s])
    n_blk = -(-n_assign // MOE_BLOCK) + N_EXPERTS
    n_rows = n_blk * MOE_BLOCK
    row_tok = jnp.full((n_rows,), T, jnp.int32).at[dest].set(tok_s)
    row_w = jnp.zeros((n_rows,), jnp.float32).at[dest].set(w_s)
    blk_expert = jnp.minimum(jnp.searchsorted(pends, jnp.arange(n_blk) * MOE_BLOCK, side='right'), N_EXPERTS - 1)
    x_pad = jnp.concatenate([xt, jnp.zeros((1, D), xt.dtype)], axis=0)

    def expert_block(args):
        toks, e = args
        xb = x_pad[toks]
        h = jax.nn.silu(xb @ w_gate[e]) * (xb @ w_up[e])
        return h @ w_down[e]

    y_rows = lax.map(expert_block, (row_tok.reshape(n_blk, MOE_BLOCK), blk_expert))
    y_rows = y_rows.reshape(n_rows, D) * row_w[:, None].astype(x.dtype)
    y = jnp.zeros((T + 1, D), x.dtype).at[row_tok].add(y_rows)
    return y[:T].reshape(B, S, D)


def setup_inputs(seed: int = 0) -> dict:
    key = jax.random.key(seed)
    ks = jax.random.split(key, 20)
    nrm = lambda k, shape, s: jax.random.normal(k, shape, jnp.float32) * s
    sd = D_MODEL ** -0.5
    col_scale_moba = jnp.concatenate([jnp.ones((2 * D_MODEL,)), jnp.full((D_MODEL,), DEEPNORM_BETA)]).astype(jnp.float32)
    col_scale_hgrn = jnp.concatenate([jnp.ones((2 * D_MODEL,)), jnp.full((D_MODEL,), DEEPNORM_BETA),
                                      jnp.ones((D_MODEL,))]).astype(jnp.float32)
    return {
        "x": nrm(ks[0], (BATCH, SEQ, D_MODEL), 1.0),
        "moba_w_in": nrm(ks[1], (N_MOBA_LAYERS, D_MODEL, 3 * D_MODEL), sd) * col_scale_moba,
        "moba_w_o": nrm(ks[2], (N_MOBA_LAYERS, D_MODEL, D_MODEL), sd * DEEPNORM_BETA),
        "hgrn_w_in": nrm(ks[3], (N_HGRN_LAYERS, D_MODEL, 4 * D_MODEL), sd) * col_scale_hgrn,
        "hgrn_g_norm": 1.0 + nrm(ks[4], (N_HGRN_LAYERS, HEAD_DIM), 0.02),
        "hgrn_lb_raw": nrm(ks[5], (N_HGRN_LAYERS, D_MODEL), 0.1),
        "hgrn_w_o": nrm(ks[6], (N_HGRN_LAYERS, D_MODEL, D_MODEL), sd * DEEPNORM_BETA),
        "ln_mix_g": 1.0 + nrm(ks[7], (DEPTH, D_MODEL), 0.02),
        "ln_mix_b": nrm(ks[8], (DEPTH, D_MODEL), 0.02),
        "moe_w_rg": nrm(ks[9], (DEPTH, D_MODEL, N_GROUPS), sd),
        "moe_b_rg": nrm(ks[10], (DEPTH, N_GROUPS), 0.01),
        "moe_w_re": nrm(ks[11], (DEPTH, N_GROUPS, D_MODEL, EXPERTS_PER_GROUP), sd),
        "moe_b_re": nrm(ks[12], (DEPTH, N_GROUPS, EXPERTS_PER_GROUP), 0.01),
        "moe_w_gate": nrm(ks[13], (DEPTH, N_EXPERTS, D_MODEL, D_EXPERT), sd),
        "moe_w_up": nrm(ks[14], (DEPTH, N_EXPERTS, D_MODEL, D_EXPERT), sd * DEEPNORM_BETA),
        "moe_w_down": nrm(ks[15], (DEPTH, N_EXPERTS, D_EXPERT, D_MODEL), D_EXPERT ** -0.5 * DEEPNORM_BETA),
        "ln_ffn_g": 1.0 + nrm(ks[16], (DEPTH, D_MODEL), 0.02),
        "ln_ffn_b": nrm(ks[17], (DEPTH, D_MODEL), 0.02),
    }


def reference(x, moba_w_in, moba_w_o, hgrn_w_in, hgrn_g_norm, hgrn_lb_raw, hgrn_w_o, ln_mix_g, ln_mix_b,
              moe_w_rg, moe_b_rg, moe_w_re, moe_b_re, moe_w_gate, moe_w_up, moe_w_down, ln_ffn_g, ln_ffn_b):
    lb_all = jnp.cumsum(jax.nn.softmax(hgrn_lb_raw.astype(jnp.float32), axis=0), axis=0)
    lb_all = lb_all - lb_all[0]
    for layer in range(DEPTH):
        j = layer // N_MIXERS
        if layer % N_MIXERS == 0:
            h = moba_attention(x, moba_w_in[j], moba_w_o[j])
        else:
            h = hgrn2(x, hgrn_w_in[j], hgrn_g_norm[j], lb_all[j], hgrn_w_o[j])
        x = layer_norm(DEEPNORM_ALPHA * x + h, ln_mix_g[layer], ln_mix_b[layer])
        m = hierarchical_moe(x, moe_w_rg[layer], moe_b_rg[layer], moe_w_re[layer], moe_b_re[layer],
                             moe_w_gate[layer], moe_w_up[layer], moe_w_down[layer])
        x = layer_norm(DEEPNORM_ALPHA * x + m, ln_ffn_g[layer], ln_ffn_b[layer])
    return x
```

```python
from contextlib import ExitStack
import numpy as np
import ml_dtypes
import concourse.bass as bass
import concourse.mybir as mybir
from concourse.bass_utils import run_bass_kernel_spmd

F32 = mybir.dt.float32
BF16 = mybir.dt.bfloat16
I32 = mybir.dt.int32
AF = mybir.ActivationFunctionType
ALU = mybir.AluOpType
AX = mybir.AxisListType

ENGS = ["pe", "act", "dve", "pool", "sp"]
NDMA = 6

D = 2048
T = 2048
NT = 16
DC = 16
NH = 16
DEPTH = 4
ALPHA = (2 * DEPTH) ** 0.25
LN_EPS = 1e-5
RMS_EPS = 1e-6
CAP = 256
NE = 32
NSLOT = NE * CAP
SCALE = 128 ** -0.5
NEG = -1.0e30
PAIRS = [[0, 1], [2, 3], [4, 5], [6, 7]]
HSTOP = 0
NOCC = 0
WMODE = "repl"


class Prog:
    def __init__(self, nc, stack):
        self.nc = nc
        self.ops = {e: [] for e in ENGS}
        self.sem = {e: stack.enter_context(nc.semaphore("s_" + e)) for e in ENGS}
        self.cnt = {e: 0 for e in ENGS}
        self.dsem = {e: [stack.enter_context(nc.semaphore("d_%s%d" % (e, i))) for i in range(NDMA)]
                     for e in ("sp", "act", "pool")}
        self.dcnt = {e: 0 for e in ("sp", "act", "pool")}
        self.ccsem = stack.enter_context(nc.semaphore("s_cc"))
        self.cccnt = 0
        self.sems = {("cc",): self.ccsem}
        for e in ENGS:
            self.sems[("c", e)] = self.sem[e]
        for e in self.dsem:
            for i, s in enumerate(self.dsem[e]):
                self.sems[("d", e, i)] = s
        self.seen = {e: {} for e in ENGS}
        self.last_w = {}
        self.readers = {}

    def _need(self, eng, tok, waits):
        if tok is None:
            return
        sid, val, teng = tok
        if eng == "pe" and teng == "pe" and sid[0] == "c":
            return
        if self.seen[eng].get(sid, 0) >= val:
            return
        self.seen[eng][sid] = val
        waits.append((sid, val))

    def _deps(self, eng, r, w, waits):
        for k in r:
            self._need(eng, self.last_w.get(k), waits)
        for k in w:
            self._need(eng, self.last_w.get(k), waits)
            for t in self.readers.get(k, ()):
                self._need(eng, t, waits)

    def _commit(self, tok, r, w):
        for k in r:
            self.readers.setdefault(k, []).append(tok)
        for k in w:
            self.last_w[k] = tok
            self.readers[k] = []

    def op(self, eng, fn, r=(), w=()):
        waits = []
        self._deps(eng, r, w, waits)
        self.cnt[eng] += 1
        tok = (("c", eng), self.cnt[eng], eng)
        self.ops[eng].append((waits, fn, ("c", eng), 1))
        self._commit(tok, r, w)

    def dma(self, q, fn, r=(), w=()):
        waits = []
        self._deps(q, r, w, waits)
        i = self.dcnt[q]
        self.dcnt[q] += 1
        slot = i % NDMA
        sid = ("d", q, slot)
        prev = (i // NDMA) * 16
        if prev > 0:
            self._need(q, (sid, prev, "dma"), waits)
        tok = (sid, prev + 16, "dma")
        self.ops[q].append((waits, fn, sid, 16))
        self._commit(tok, r, w)

    def cc(self, fn, r=(), w=()):
        waits = []
        self._deps("pool", r, w, waits)
        self.cccnt += 1
        tok = (("cc",), self.cccnt, "dma")
        self.ops["pool"].append((waits, fn, ("cc",), None))
        self._commit(tok, r, w)

    def wait_all(self, eng):
        waits = []
        for e in ENGS:
            if self.cnt[e]:
                self._need(eng, (("c", e), self.cnt[e], "x"), waits)
        for q in self.dsem:
            n = self.dcnt[q]
            for slot in range(NDMA):
                k = (n - slot + NDMA - 1) // NDMA if n > slot else 0
                if k:
                    self._need(eng, (("d", q, slot), 16 * k, "dma"), waits)
        if self.cccnt:
            self._need(eng, (("cc",), self.cccnt, "dma"), waits)
        self.ops[eng].append((waits, None, None, 0))

    def barrier(self):
        for e in ENGS:
            self.wait_all(e)
        self.last_w = {}
        self.readers = {}

    def emit(self):
        nc = self.nc
        with nc.Block() as block:
            def run(e, name):
                for waits, fn, sid, amt in self.ops[name]:
                    for (ws, wv) in waits:
                        e.wait_ge(self.sems[ws], wv)
                    if fn is not None:
                        ins = fn(e)
                        if amt is None:
                            ins.then_inc(self.sems[sid])
                        else:
                            ins.then_inc(self.sems[sid], amt)

            @block.tensor
            def _(e):
                run(e, "pe")

            @block.scalar
            def _(e):
                run(e, "act")

            @block.vector
            def _(e):
                run(e, "dve")

            @block.gpsimd
            def _(e):
                run(e, "pool")

            @block.sync
            def _(e):
                run(e, "sp")


class Arena:
    def __init__(self, nc, stack, nbytes):
        self.t = stack.enter_context(nc.sbuf_tensor("arena", [128, nbytes // 2], BF16))
        self.n = nbytes
        self.off = 0

    def alloc(self, free_shape, dt):
        esz = 4 if dt in (F32, I32) else 2
        n = int(np.prod(free_shape)) * esz
        self.off = (self.off + 63) // 64 * 64
        assert self.off + n <= self.n, ("SBUF arena overflow", self.off, n, self.n)
        a = self.t[:, self.off // 2:(self.off + n) // 2]
        if esz == 4:
            a = a.bitcast(dt)
        self.off += n
        if len(free_shape) == 2:
            a = a.rearrange("p (a b) -> p a b", b=free_shape[1])
        elif len(free_shape) == 3:
            a = a.rearrange("p (a b c) -> p a b c", b=free_shape[1], c=free_shape[2])
        return a

    def mark(self):
        return self.off

    def release(self, m):
        self.off = m


class K:
    def __init__(self, n_layers=DEPTH, dbg=False, only=None):
        self.only = only
        self.n_layers = n_layers
        self.dbg = dbg
        self.nc = bass.Bass("TRN2", target_bir_lowering=False)

    def mm(self, out, lhsT, rhs, start, stop, r, w):
        self.p.op("pe", lambda e: e.matmul(out, lhsT=lhsT, rhs=rhs, start=start, stop=stop), r=r, w=w)

    def tr(self, out, in_, ident, r, w):
        self.p.op("pe", lambda e: e.transpose(out, in_, ident), r=r, w=w)

    def act(self, out, in_, func, r, w, **kw):
        self.p.op("act", lambda e: e.activation(out=out, in_=in_, func=func, **kw), r=r, w=w)

    def cp(self, eng, out, in_, r, w):
        if eng == "act":
            self.p.op("act", lambda e: e.copy(out=out, in_=in_), r=r, w=w)
        else:
            self.p.op(eng, lambda e: e.tensor_copy(out=out, in_=in_), r=r, w=w)

    def tt(self, eng, out, in0, in1, op, r, w):
        self.p.op(eng, lambda e: e.tensor_tensor(out=out, in0=in0, in1=in1, op=op), r=r, w=w)

    def ts(self, eng, out, in0, s1, s2, op0, r, w, op1=None):
        if op1 is None:
            self.p.op(eng, lambda e: e.tensor_scalar(out=out, in0=in0, scalar1=s1, scalar2=None, op0=op0), r=r, w=w)
        else:
            self.p.op(eng, lambda e: e.tensor_scalar(out=out, in0=in0, scalar1=s1, scalar2=s2, op0=op0, op1=op1),
                      r=r, w=w)

    def stt(self, eng, out, in0, scalar, in1, op0, op1, r, w):
        self.p.op(eng, lambda e: e.scalar_tensor_tensor(out=out, in0=in0, scalar=scalar, in1=in1, op0=op0, op1=op1),
                  r=r, w=w)

    def dma(self, q, out, in_, r, w):
        self.p.dma(q, lambda e: e.dma_start(out=out, in_=in_), r=r, w=w)

    def memset(self, eng, ap, val, w):
        self.p.op(eng, lambda e: e.memset(ap, val), w=w)

    def build_layer(self, kind, j, mixer_only=False):
        nc = self.nc
        di = lambda name, shape, dt=F32: nc.dram_tensor(name, shape, dt, kind="ExternalInput").ap()
        self.x_in = di("x", [T, D])
        self.gath = []
        if kind == "moba":
            w = di("w_in", [D, 3 * D])
            self.moba_w_in = [w, w]
        else:
            w = di("w_in", [D, 4 * D])
            self.hgrn_w_in = [w, w]
            gnm = di("g_norm", [128])
            self.hgrn_g_norm = [gnm, gnm]
            self.hgrn_lb_raw = di("lb_raw", [2, D])
        if not mixer_only:
            wo = di("w_o", [D, D])
            rep = lambda a: [a] * DEPTH
            self.ln_mix_g = rep(di("ln_mix_g", [D]))
            self.ln_mix_b = rep(di("ln_mix_b", [D]))
            self.moe_w_rg = rep(di("moe_w_rg", [D, 4]))
            self.moe_b_rg = rep(di("moe_b_rg", [4]))
            self.moe_w_re = rep(di("moe_w_re", [4, D, 8]))
            self.moe_b_re = rep(di("moe_b_re", [32]))
            self.moe_w_gate = rep(di("moe_w_gate", [NE, D, 512]))
            self.moe_w_up = rep(di("moe_w_up", [NE, D, 512]))
            self.moe_w_down = rep(di("moe_w_down", [NE, 512, D]))
            self.ln_ffn_g = rep(di("ln_ffn_g", [D]))
            self.ln_ffn_b = rep(di("ln_ffn_b", [D]))
        self.c_identb = di("c_identb", [128, 128], BF16)
        self.c_identf = di("c_identf", [128, 128])
        self.c_trile = di("c_trile", [128, 128], BF16)
        self.c_btri = di("c_btri", [128, 128], BF16)
        self.c_tstrict = di("c_tstrict", [128, 128])
        self.c_ones = di("c_ones", [128, 128])
        self.c_eoff = di("c_eoff", [128, 32])
        self.c_gbias = di("c_gbias", [128, 256])
        self.c_hflag = di("c_hflag", [128, 1])
        self.c_pmask = di("c_pmask", [128, 2])
        dt = lambda name, shape, dtp=F32: nc.dram_tensor(name, shape, dtp).ap()
        if mixer_only:
            self.o_dram = nc.dram_tensor("o_out", [NH * T, 128], BF16, kind="ExternalOutput").ap()
        else:
            self.out = nc.dram_tensor("out", [T, D], F32, kind="ExternalOutput").ap()
            self.o_dram = dt("o_dram", [NH * T, 128], BF16)
        self.xa = dt("xa", [T, D])
        self.x1 = dt("x1", [T, D])
        self.qT = dt("qT", [NH * 128, T], BF16)
        self.kT_own = dt("kT_own", [NH * 128, T], BF16)
        self.kT_all = dt("kT_all", [2 * NH * 128, T], BF16)
        self.v_own = dt("v_own", [NH * T, 128], BF16)
        self.v_all = dt("v_all", [2 * NH * T, 128], BF16)
        self.xg = dt("xg", [NSLOT + 128, D], BF16)
        self.yg = dt("yg", [NSLOT + 128, D])
        self.s_own = dt("s_own", [256, 1024])
        self.s_all = dt("s_all", [512, 1024])
        with ExitStack() as st:
            self.st = st
            self.p = Prog(nc, st)
            self.ar = Arena(nc, st, 207 * 1024)
            self.ps = [st.enter_context(nc.psum_tensor("ps%d" % i, [128, 512], F32)) for i in range(8)]
            self.consts()
            if kind == "moba":
                self.moba_proj(self.x_in, j)
                self.moba_attn()
            else:
                self.hgrn_layer(self.x_in, j)
            if not mixer_only:
                self.wo_ln_router(self.x_in, wo, 0)
                self.experts(0)
                self.combine_ln(0, self.out)
            self.p.barrier()
            self.p.emit()
        return nc

    def build(self):
        nc = self.nc
        di = lambda name, shape, dt=F32: nc.dram_tensor(name, shape, dt, kind="ExternalInput").ap()
        self.x_in = di("x", [T, D])
        self.gath = []
        def big(name, shape):
            pieces = shape[0]
            if WMODE == "fake":
                per_ = int(np.prod(shape[1:-1]))
                return [nc.dram_tensor(name + "_f%d" % i, [per_, shape[-1]], F32).ap() for i in range(pieces)]
            if WMODE == "repl":
                per_ = int(np.prod(shape[1:-1]))
                full_ = di(name, [pieces * per_, shape[-1]])
                return [full_[i * per_:(i + 1) * per_, :] for i in range(pieces)]
            R = int(np.prod(shape[:-1])); C = shape[-1]
            per = R // pieces
            ext = di(name, [R // 8, C])
            stg = [nc.dram_tensor(name + "_stg%d" % i, [per // 8, C], F32).ap() for i in range(pieces)]
            full = [nc.dram_tensor(name + "_full%d" % i, [per, C], F32).ap() for i in range(pieces)]
            self.gath.append((ext, stg, full, pieces, per))
            return full
        self.moba_w_in = big("moba_w_in", [2, D, 3 * D])
        self.moba_w_o = big("moba_w_o", [2, D, D])
        self.hgrn_w_in = big("hgrn_w_in", [2, D, 4 * D])
        self.hgrn_g_norm = di("hgrn_g_norm", [2, 128])
        self.hgrn_lb_raw = di("hgrn_lb_raw", [2, D])
        self.hgrn_w_o = big("hgrn_w_o", [2, D, D])
        self.ln_mix_g = di("ln_mix_g", [DEPTH, D])
        self.ln_mix_b = di("ln_mix_b", [DEPTH, D])
        self.moe_w_rg = di("moe_w_rg", [DEPTH, D, 4])
        self.moe_b_rg = di("moe_b_rg", [DEPTH, 4])
        self.moe_w_re = di("moe_w_re", [DEPTH, 4, D, 8])
        self.moe_b_re = di("moe_b_re", [DEPTH, 32])
        self.moe_w_gate = [f.rearrange("(e r) c -> e r c", e=NE) for f in big("moe_w_gate", [DEPTH, NE, D, 512])]
        self.moe_w_up = [f.rearrange("(e r) c -> e r c", e=NE) for f in big("moe_w_up", [DEPTH, NE, D, 512])]
        self.moe_w_down = [f.rearrange("(e r) c -> e r c", e=NE) for f in big("moe_w_down", [DEPTH, NE, 512, D])]
        self.ln_ffn_g = di("ln_ffn_g", [DEPTH, D])
        self.ln_ffn_b = di("ln_ffn_b", [DEPTH, D])
        self.c_identb = di("c_identb", [128, 128], BF16)
        self.c_identf = di("c_identf", [128, 128])
        self.c_trile = di("c_trile", [128, 128], BF16)
        self.c_btri = di("c_btri", [128, 128], BF16)
        self.c_tstrict = di("c_tstrict", [128, 128])
        self.c_ones = di("c_ones", [128, 128])
        self.c_eoff = di("c_eoff", [128, 32])
        self.c_gbias = di("c_gbias", [128, 256])
        self.c_hflag = di("c_hflag", [128, 1])
        self.c_pmask = di("c_pmask", [128, 2])
        self.out = nc.dram_tensor("out", [T, D], F32, kind="ExternalOutput").ap()
        dt = lambda name, shape, dtp=F32: nc.dram_tensor(name, shape, dtp).ap()
        self.xa = dt("xa", [T, D])
        self.x1 = dt("x1", [T, D])
        self.qT = dt("qT", [NH * 128, T], BF16)
        self.kT_own = dt("kT_own", [NH * 128, T], BF16)
        self.kT_all = dt("kT_all", [2 * NH * 128, T], BF16)
        self.v_own = dt("v_own", [NH * T, 128], BF16)
        self.v_all = dt("v_all", [2 * NH * T, 128], BF16)
        self.o_dram = dt("o_dram", [NH * T, 128], BF16)
        self.xg = dt("xg", [NSLOT + 128, D], BF16)
        self.yg = dt("yg", [NSLOT + 128, D])
        self.s_own = dt("s_own", [256, 1024])
        self.s_all = dt("s_all", [512, 1024])
        self.dbgs = {}
        with ExitStack() as st:
            self.st = st
            self.p = Prog(nc, st)
            self.ar = Arena(nc, st, 207 * 1024)
            self.ps = [st.enter_context(nc.psum_tensor("ps%d" % i, [128, 512], F32)) for i in range(8)]
            self.consts()
            self.gather_weights()
            x_src = self.x_in
            for l in range(self.n_layers):
                j = l // 2
                last = (l == self.n_layers - 1)
                on = lambda ph: self.only is None or ph in self.only
                if l % 2 == 0:
                    if on("proj"):
                        self.moba_proj(x_src, j)
                    if on("attn"):
                        self.moba_attn()
                    w_o = self.moba_w_o[j]
                else:
                    self.hgrn_layer(x_src, j)
                    w_o = self.hgrn_w_o[j]
                if on("wo"):
                    self.wo_ln_router(x_src, w_o, l)
                if on("exp"):
                    self.experts(l)
                if on("comb"):
                    self.combine_ln(l, self.out if last else self.xa)
                else:
                    self.dma("sp", self.out[0:128, :], self.x_in[0:128, :], r=[], w=["o"])
                x_src = self.xa
            self.p.barrier()
            self.p.emit()
        return nc

    def consts(self):
        ar = self.ar
        self.identb = ar.alloc([128], BF16)
        self.identf = ar.alloc([128], F32)
        self.trile = ar.alloc([128], BF16)
        self.btri = ar.alloc([128], BF16)
        self.tstrict = ar.alloc([128], F32)
        self.ones = ar.alloc([128], F32)
        self.eoff = ar.alloc([32], F32)
        self.gbias = ar.alloc([256], F32)
        self.hflag = ar.alloc([1], F32)
        self.pmask = ar.alloc([2], F32)
        self.slots = ar.alloc([NT, 2], I32)
        self.wts = ar.alloc([NT, 2], F32)
        self.epsln = ar.alloc([1], F32)
        self.epsrms = ar.alloc([1], F32)
        self.memset("dve", self.epsln, LN_EPS, w=["const"])
        self.memset("dve", self.epsrms, RMS_EPS, w=["const"])
        for sb, dr in ((self.identb, self.c_identb), (self.identf, self.c_identf), (self.trile, self.c_trile),
                       (self.btri, self.c_btri), (self.tstrict, self.c_tstrict), (self.ones, self.c_ones),
                       (self.eoff, self.c_eoff), (self.gbias, self.c_gbias), (self.hflag, self.c_hflag),
                       (self.pmask, self.c_pmask)):
            self.dma("sp", sb, dr, r=[], w=["const"])
        self.p.barrier()

    def gather_weights(self):
        for gi, (ext, stg, full, pieces, per) in enumerate(self.gath):
            ps_ = per // 8
            for i in range(pieces):
                self.dma("sp", stg[i], ext[i * ps_:(i + 1) * ps_, :], r=[], w=[("wstg", gi, i)])
        self.p.barrier()
        for gi, (ext, stg, full, pieces, per) in enumerate(self.gath):
            for i in range(pieces):
                self.p.cc(lambda e, a=stg[i], b=full[i]: e.collective_compute(
                    "AllGather", ALU.bypass, replica_groups=[list(range(8))],
                    ins=[a], outs=[b]), r=[], w=[("wfull", gi, i)])
        self.p.barrier()

    def psb(self, i):
        return self.ps[i][:, :].bitcast(BF16)

    def build_xT(self, x_src, xT):
        ar = self.ar
        m = ar.mark()
        xin = [ar.alloc([D], F32) for _ in range(2)]
        xb = [ar.alloc([D], BF16) for _ in range(2)]
        for t in range(NT):
            s = t % 2
            self.dma("sp", xin[s], x_src[t * 128:(t + 1) * 128, :], r=[], w=[("xin", s)])
            self.cp("act", xb[s], xin[s], r=[("xin", s)], w=[("xb", s)])
            for half in range(2):
                bank = self.psb(half)
                for c8 in range(8):
                    c = half * 8 + c8
                    self.tr(bank[:, c8 * 128:(c8 + 1) * 128], xb[s][:, c * 128:(c + 1) * 128], self.identb,
                            r=[("xb", s)], w=[("ps", half)])
                self.cp("dve" if half == 0 else "pool" if False else "dve",
                        xT[:, half * 8:(half + 1) * 8, t * 128:(t + 1) * 128],
                        bank.rearrange("p (a b) -> p a b", b=128), r=[("ps", half)], w=[("xT", t)])
        self.p.barrier()
        ar.release(m)

    def load_w(self, wt, src, key):
        self.dma("pool", wt, src.rearrange("(c p) n -> p c n", p=128), r=[], w=[key])

    def moba_proj(self, x_src, j):
        ar = self.ar
        m0 = ar.mark()
        xT = ar.alloc([DC, T], BF16)
        self.build_xT(x_src, xT)
        wb = [ar.alloc([DC, 512], BF16) for _ in range(2)]
        stg = [ar.alloc([T], BF16) for _ in range(2)]
        vst = [ar.alloc([512], BF16) for _ in range(2)]
        w_in = self.moba_w_in[j]
        xT_keys = [("xT", t) for t in range(NT)]
        nb = 0
        wi = 0
        for cg in range(8):
            wt = wb[wi % 2]
            wkey = ("wb", wi % 2)
            wi += 1
            self.load_w(wt, w_in[:, cg * 512:(cg + 1) * 512], wkey)
            for hh in range(4):
                head = (cg % 4) * 4 + hh
                s = hh % 2
                for tb in range(4):
                    b = 2 + nb % 4
                    nb += 1
                    bank = self.ps[b]
                    for c in range(DC):
                        self.mm(bank[:, :], wt[:, c, hh * 128:(hh + 1) * 128], xT[:, c, tb * 512:(tb + 1) * 512],
                                c == 0, c == DC - 1, r=[wkey] + xT_keys[tb * 4:(tb + 1) * 4], w=[("ps", b)])
                    self.cp("act" if nb % 2 else "dve", stg[s][:, tb * 512:(tb + 1) * 512], bank[:, :],
                            r=[("ps", b)], w=[("stg", s, tb)])
                dst = self.qT if cg < 4 else self.kT_own
                self.dma("sp", dst[head * 128:(head + 1) * 128, :], stg[s],
                         r=[("stg", s, tb) for tb in range(4)], w=[("qk", cg < 4, head)])
        v_view = self.v_own.rearrange("(h t p) d -> t p h d", h=NH, p=128)
        for cg in range(4):
            wt = wb[wi % 2]
            wkey = ("wb", wi % 2)
            wi += 1
            self.load_w(wt, w_in[:, 2 * D + cg * 512:2 * D + (cg + 1) * 512], wkey)
            for t in range(NT):
                b = 2 + nb % 4
                nb += 1
                bank = self.ps[b]
                for c in range(DC):
                    self.mm(bank[:, :], xT[:, c, t * 128:(t + 1) * 128], wt[:, c, :], c == 0, c == DC - 1,
                            r=[wkey, ("xT", t)], w=[("ps", b)])
                s = t % 2
                self.cp("act" if nb % 2 else "dve", vst[s], bank[:, :], r=[("ps", b)], w=[("vst", s)])
                self.dma("sp", v_view[t][:, cg * 4:(cg + 1) * 4, :], vst[s].rearrange("p (h d) -> p h d", d=128),
                         r=[("vst", s)], w=[("v", cg, t)])
        self.p.barrier()
        if self.only is not None and "cc" not in self.only:
            ar.release(m0)
            return
        for h in range(NH):
            self.p.cc(lambda e, h=h: e.collective_compute(
                "AllGather", ALU.bypass, replica_groups=PAIRS, ins=[self.kT_own[h * 128:(h + 1) * 128, :]],
                outs=[self.kT_all[h * 256:(h + 1) * 256, :]]), r=[], w=[("kT_all", h)])
            self.p.cc(lambda e, h=h: e.collective_compute(
                "AllGather", ALU.bypass, replica_groups=PAIRS, ins=[self.v_own[h * T:(h + 1) * T, :]],
                outs=[self.v_all[h * 2 * T:(h + 1) * 2 * T, :]]), r=[], w=[("v_all", h)])
        self.p.barrier()
        ar.release(m0)

    def moba_attn(self):
        ar = self.ar
        m0 = ar.mark()
        qh = [ar.alloc([T], BF16) for _ in range(2)]
        kf = [ar.alloc([T], BF16) for _ in range(2)]
        ko = [ar.alloc([T], BF16) for _ in range(2)]
        vf = [ar.alloc([NT, 130], BF16) for _ in range(2)]
        vo = [ar.alloc([NT, 130], BF16) for _ in range(2)]
        oh = [ar.alloc([NT, 128], BF16) for _ in range(2)]
        km = ar.alloc([16], F32)
        kmhi = ar.alloc([16], BF16)
        kmhf = ar.alloc([16], F32)
        kmlo = ar.alloc([16], BF16)
        gm = ar.alloc([16, 16], F32)
        top8 = ar.alloc([16, 8], F32)
        thr = ar.alloc([16], F32)
        sel = ar.alloc([16, 16], F32)
        pT = [ar.alloc([2, 256], BF16) for _ in range(2)]
        acc = [ar.alloc([2, 130], F32) for _ in range(2)]
        rec = ar.alloc([2], F32)
        for s in range(2):
            self.memset("pool", vf[s][:, :, 128:130], 1.0, w=[("vf", s)])
            self.memset("pool", vo[s][:, :, 128:130], 1.0, w=[("vo", s)])
        u = 0
        for h in range(NH):
            s = h % 2
            hk = lambda n: (n, s)
            self.dma("sp", qh[s], self.qT[h * 128:(h + 1) * 128, :], r=[], w=[hk("qh")])
            self.dma("sp", kf[s], self.kT_all[h * 256:h * 256 + 128, :], r=[], w=[hk("kf")])
            self.dma("sp", ko[s], self.kT_own[h * 128:(h + 1) * 128, :], r=[], w=[hk("ko")])
            self.dma("sp", vf[s][:, :, 0:128],
                     self.v_all[h * 2 * T:h * 2 * T + T, :].rearrange("(t p) d -> p t d", p=128), r=[], w=[hk("vf")])
            self.dma("sp", vo[s][:, :, 0:128],
                     self.v_own[h * T:(h + 1) * T, :].rearrange("(t p) d -> p t d", p=128), r=[], w=[hk("vo")])
            self.p.op("dve", lambda e, s=s: e.tensor_reduce(out=km[:, 0:8], in_=kf[s].rearrange("p (b k) -> p b k", k=256),
                                                          axis=AX.X, op=ALU.add), r=[hk("kf")], w=["km"])
            self.p.op("dve", lambda e, s=s: e.tensor_reduce(out=km[:, 8:16], in_=ko[s].rearrange("p (b k) -> p b k", k=256),
                                                          axis=AX.X, op=ALU.add), r=[hk("ko")], w=["km"])
            self.ts("dve", km, km, 1.0 / 256, None, ALU.mult, r=["km"], w=["km"])
            self.cp("dve", kmhi, km, r=["km"], w=["kmhi"])
            self.cp("dve", kmhf, kmhi, r=["kmhi"], w=["kmhf"])
            self.tt("dve", kmlo, km, kmhf, ALU.subtract, r=["km", "kmhf"], w=["kmlo"])
            gb = self.ps[6]
            for qt in range(NT):
                self.mm(gb[:, qt * 16:(qt + 1) * 16], qh[s][:, qt * 128:(qt + 1) * 128], kmhi, True, False,
                        r=[hk("qh"), "kmhi"], w=[("ps", 6)])
                self.mm(gb[:, qt * 16:(qt + 1) * 16], qh[s][:, qt * 128:(qt + 1) * 128], kmlo, False, True,
                        r=[hk("qh"), "kmlo"], w=[("ps", 6)])
            gm2 = gm.rearrange("p a b -> p (a b)")
            self.tt("dve", gm2, gb[:, 0:256], self.gbias, ALU.add, r=[("ps", 6)], w=["gm"])
            for qt in range(NT):
                self.p.op("dve", lambda e, qt=qt: e.max(out=top8[:, qt, :], in_=gm[:, qt, :]), r=["gm"], w=["top8"])
            self.ts("dve", thr, top8[:, :, 2], -1.0e29, None, ALU.max, r=["top8"], w=["thr"])
            self.tt("dve", sel, gm, thr.unsqueeze(2).to_broadcast([128, 16, 16]), ALU.is_ge,
                    r=["gm", "thr"], w=["sel"])
            for jb in range(8):
                kbl = [("f", pp) for pp in range(8)] + [("o", pp) for pp in range(jb)] + [("own", jb)]
                a = acc[jb % 2]
                akey = ("acc", jb % 2)
                qs = qh[s][:, jb * 256:(jb + 1) * 256]
                for bi, (kind, pp) in enumerate(kbl):
                    ksrc = kf[s] if kind == "f" else ko[s]
                    vsrc = vf[s] if kind == "f" else vo[s]
                    kkey = hk("kf") if kind == "f" else hk("ko")
                    vkey = hk("vf") if kind == "f" else hk("vo")
                    own = kind == "own"
                    sb_i = u % 2
                    ob_i = 2 + u % 2
                    pt = pT[u % 2]
                    pkey = ("pT", u % 2)
                    u += 1
                    sbank = self.ps[sb_i][:, :].rearrange("p (a b) -> p a b", b=256)
                    obank = self.ps[ob_i][:, :].rearrange("p (a b) -> p a b", b=256)
                    self.mm(sbank[:, 0, :], ksrc[:, pp * 256:pp * 256 + 128], qs, True, True,
                            r=[kkey, hk("qh")], w=[("ps", sb_i)])
                    if own:
                        self.mm(sbank[:, 1, 128:256], ksrc[:, pp * 256 + 128:pp * 256 + 256], qs[:, 128:256], True, True,
                                r=[kkey, hk("qh")], w=[("ps", sb_i)])
                        self.act(pt[:, 0, :], sbank[:, 0, :], AF.Exp, r=[("ps", sb_i)], w=[pkey], scale=SCALE)
                        self.act(pt[:, 1, 128:256], sbank[:, 1, 128:256], AF.Exp, r=[("ps", sb_i)], w=[pkey], scale=SCALE)
                        self.tt("pool", pt[:, 0, 0:128], pt[:, 0, 0:128], self.trile, ALU.mult, r=[pkey], w=[pkey])
                        self.tt("pool", pt[:, 1, 128:256], pt[:, 1, 128:256], self.trile, ALU.mult, r=[pkey], w=[pkey])
                    else:
                        self.mm(sbank[:, 1, :], ksrc[:, pp * 256 + 128:pp * 256 + 256], qs, True, True,
                                r=[kkey, hk("qh")], w=[("ps", sb_i)])
                        self.act(pt, sbank, AF.Exp, r=[("ps", sb_i)], w=[pkey], scale=SCALE)
                    for qi in range(2):
                        kts = [0] if (own and qi == 0) else [0, 1]
                        for n, kt in enumerate(kts):
                            self.mm(obank[:, qi, 0:129], pt[:, kt, qi * 128:(qi + 1) * 128], vsrc[:, pp * 2 + kt, 0:129],
                                    n == 0, n == len(kts) - 1, r=[pkey, vkey], w=[("ps", ob_i)])
                    for qi in range(2):
                        qt = 2 * jb + qi
                        if own:
                            self.tt("dve", a[:, qi, 0:129], obank[:, qi, 0:129], a[:, qi, 0:129], ALU.add,
                                    r=[("ps", ob_i), akey], w=[akey])
                        else:
                            idx = pp if kind == "f" else 8 + pp
                            if bi == 0:
                                self.ts("dve", a[:, qi, 0:129], obank[:, qi, 0:129], sel[:, qt, idx:idx + 1], None,
                                        ALU.mult, r=[("ps", ob_i), "sel", akey], w=[akey])
                            else:
                                self.stt("dve", a[:, qi, 0:129], obank[:, qi, 0:129], sel[:, qt, idx:idx + 1],
                                         a[:, qi, 0:129], ALU.mult, ALU.add, r=[("ps", ob_i), "sel", akey], w=[akey])
                self.p.op("dve", lambda e, a=a: e.reciprocal(out=rec, in_=a[:, :, 128]), r=[akey], w=["rec"])
                for qi in range(2):
                    self.ts("dve", oh[s][:, 2 * jb + qi, :], a[:, qi, 0:128], rec[:, qi:qi + 1], None, ALU.mult,
                            r=[akey, "rec"], w=[hk("oh")])
            self.dma("sp", self.o_dram[h * T:(h + 1) * T, :].rearrange("(t p) d -> p t d", p=128), oh[s],
                     r=[hk("oh")], w=[("o_dram", h)])
        self.p.barrier()
        ar.release(m0)

    def hgrn_layer(self, x_src, j):
        ar = self.ar
        m0 = ar.mark()
        xT = ar.alloc([DC, T], BF16)
        self.build_xT(x_src, xT)
        w_in = self.hgrn_w_in[j]
        xT_keys = [("xT", t) for t in range(NT)]
        lbt = ar.alloc([16], F32)
        oml = ar.alloc([16], F32)
        if j == 0:
            self.memset("dve", lbt, 0.0, w=["lbt"])
            self.memset("dve", oml, 1.0, w=["oml"])
        else:
            r0 = ar.alloc([16], F32)
            r1 = ar.alloc([16], F32)
            self.p.dma("sp", lambda e: e.dma_start(out=r0, in_=self.hgrn_lb_raw[0].rearrange("(h p) -> p h", p=128),
                                                   allow_slow_non_contiguous=True), r=[], w=["r0"])
            self.p.dma("sp", lambda e: e.dma_start(out=r1, in_=self.hgrn_lb_raw[1].rearrange("(h p) -> p h", p=128),
                                                   allow_slow_non_contiguous=True), r=[], w=["r1"])
            self.tt("dve", r1, r1, r0, ALU.subtract, r=["r0", "r1"], w=["r1"])
            self.act(lbt, r1, AF.Sigmoid, r=["r1"], w=["lbt"])
            self.ts("dve", oml, lbt, -1.0, 1.0, ALU.mult, r=["lbt"], w=["oml"], op1=ALU.add)
        gn = ar.alloc([128], F32)
        self.dma("sp", gn, self.hgrn_g_norm[j].partition_broadcast(128), r=[], w=["gn"])
        onesb = ar.alloc([T], BF16)
        self.memset("dve", onesb, 1.0, w=["onesb"])
        wb = [ar.alloc([DC, 512], BF16) for _ in range(2)]
        itok = ar.alloc([NT, 512], BF16)
        sgtok = ar.alloc([NT, 512], BF16)
        A = ar.alloc([T], F32)
        B = ar.alloc([T], F32)
        Cc = ar.alloc([T], F32)
        E = ar.alloc([T], F32)
        qhat = ar.alloc([T], BF16)
        qhA = ar.alloc([NT, 128], BF16)
        qhB = ar.alloc([NT, 128], BF16)
        khat = ar.alloc([T], BF16)
        ktA = ar.alloc([NT, 128], BF16)
        ktB = ar.alloc([NT, 128], BF16)
        oh = ar.alloc([NT, 128], BF16)
        basec = ar.alloc([32], F32)
        dec = ar.alloc([32], F32)
        S = [ar.alloc([128], F32) for _ in range(2)]
        Sb = [ar.alloc([128], BF16) for _ in range(2)]
        tmpS = ar.alloc([128], F32)
        am = [ar.alloc([128], BF16) for _ in range(2)]
        sq = ar.alloc([128], F32)
        ss = ar.alloc([1], F32)
        rs = ar.alloc([1], F32)
        tmpo = ar.alloc([128], F32)
        nb = 0
        for pas in range(2):
            for hg in range(4):
                for which, dstb in ((2, itok), (3, sgtok)):
                    if which == 3 and pas == 0:
                        continue
                    wt = wb[0]
                    self.load_w(wt, w_in[:, which * D + hg * 512:which * D + (hg + 1) * 512], ("wb", 0))
                    for t in range(NT):
                        b = 2 + nb % 4
                        nb += 1
                        bank = self.ps[b]
                        for c in range(DC):
                            self.mm(bank[:, :], xT[:, c, t * 128:(t + 1) * 128], wt[:, c, :], c == 0, c == DC - 1,
                                    r=[("wb", 0), ("xT", t)], w=[("ps", b)])
                        if which == 2:
                            self.cp("dve", dstb[:, t, :], bank[:, :], r=[("ps", b)], w=[("itok", t)])
                        else:
                            self.act(dstb[:, t, :], bank[:, :], AF.Silu, r=[("ps", b)], w=[("sgtok", t)])
                self.load_w(wb[0], w_in[:, hg * 512:(hg + 1) * 512], ("wb", 0))
                self.load_w(wb[1], w_in[:, D + hg * 512:D + (hg + 1) * 512], ("wb", 1))
                for hh in range(4):
                    h = hg * 4 + hh
                    for tb in range(4):
                        for wi_, (dst, fn) in enumerate(((A, AF.Silu), (B, AF.Sigmoid))):
                            b = 2 + nb % 4
                            nb += 1
                            bank = self.ps[b]
                            for c in range(DC):
                                self.mm(bank[:, :], wb[wi_][:, c, hh * 128:(hh + 1) * 128], xT[:, c, tb * 512:(tb + 1) * 512],
                                        c == 0, c == DC - 1, r=[("wb", wi_)] + xT_keys[tb * 4:(tb + 1) * 4], w=[("ps", b)])
                            self.act(dst[:, tb * 512:(tb + 1) * 512], bank[:, :], fn, r=[("ps", b)],
                                     w=[("A" if wi_ == 0 else "B", tb)])
                    if HSTOP == 1:
                        continue
                    Ak = [("A", tb) for tb in range(4)]
                    Bk = [("B", tb) for tb in range(4)]
                    self.ts("dve", B, B, oml[:, h:h + 1], lbt[:, h:h + 1], ALU.mult, r=Bk + ["oml", "lbt"], w=["Bf"], op1=ALU.add)
                    self.ts("dve", Cc, B, -1.0, 1.0, ALU.mult, r=["Bf"], w=["Cc"], op1=ALU.add)
                    self.act(B, B, AF.Ln, r=["Bf", "Cc"], w=["Bf"])
                    self.p.op("dve", lambda e: e.tensor_tensor_scan(out=E, data0=onesb, data1=B, initial=0.0,
                                                                    op0=ALU.mult, op1=ALU.add), r=["Bf", "onesb"], w=["E"])
                    E3 = E.rearrange("p (c t) -> p c t", t=64)
                    self.memset("dve", basec[:, 0:1], 0.0, w=["basec"])
                    self.cp("dve", basec[:, 1:32], E3[:, 0:31, 63], r=["E"], w=["basec"])
                    self.tt("dve", E3, E3, basec.unsqueeze(2).to_broadcast([128, 32, 64]), ALU.subtract,
                            r=["E", "basec"], w=["E"])
                    self.act(dec, E3[:, :, 63], AF.Exp, r=["E"], w=["dec"])
                    self.act(B, E, AF.Exp, r=["E", "Bf"], w=["Bf"])
                    self.tt("dve", qhat, A, B, ALU.mult, r=Ak + ["Bf"], w=["qhat"])
                    self.act(B, E, AF.Exp, r=["E", "Bf", "qhat"], w=["Bf"], scale=-1.0)
                    self.tt("dve", khat, Cc, B, ALU.mult, r=["Cc", "Bf"], w=["khat"])
                    q3 = qhat.rearrange("p (t k) -> p t k", k=128)
                    self.cp("pool", qhA, q3, r=["qhat"], w=["qhA"])
                    self.cp("pool", qhB, q3, r=["qhat"], w=["qhB"])
                    self.memset("pool", qhA[:, :, 64:128], 0.0, w=["qhA"])
                    self.memset("pool", qhB[:, :, 0:64], 0.0, w=["qhB"])
                    for half in range(2):
                        bank = self.psb(half)
                        for c8 in range(8):
                            t = half * 8 + c8
                            self.tr(bank[:, c8 * 128:(c8 + 1) * 128], khat[:, t * 128:(t + 1) * 128], self.identb,
                                    r=["khat"], w=[("ps", half)])
                        src = bank.rearrange("p (a b) -> p a b", b=128)
                        self.ts("dve", ktA[:, half * 8:(half + 1) * 8, :], src, self.pmask[:, 0:1], None, ALU.mult,
                                r=[("ps", half)], w=["ktA"])
                        self.ts("dve", ktB[:, half * 8:(half + 1) * 8, :], src, self.pmask[:, 1:2], None, ALU.mult,
                                r=[("ps", half)], w=["ktB"])
                    if HSTOP == 2:
                        continue
                    si = 0
                    if pas == 0:
                        self.memset("dve", S[0], 0.0, w=[("S", 0)])
                    else:
                        self.dma("sp", tmpS, self.s_all[(h // 8) * 256 + (h % 8) * 16:(h // 8) * 256 + (h % 8) * 16 + 16, :].rearrange("r (a v) -> (r a) v", v=128), r=[], w=["tmpS"])
                        self.ts("dve", S[0], tmpS, self.hflag[:, 0:1], None, ALU.mult, r=["tmpS"], w=[("S", 0)])
                    self.cp("pool", Sb[0], S[0], r=[("S", 0)], w=[("Sb", 0)])
                    for t in range(NT):
                        iv = itok[:, t, hh * 128:(hh + 1) * 128]
                        ob = self.ps[7]
                        if pas == 1:
                            ab = self.ps[6]
                            amt = am[t % 2]
                            self.mm(ab[:, 0:128], khat[:, t * 128:(t + 1) * 128], qhat[:, t * 128:(t + 1) * 128], True, True,
                                    r=["khat", "qhat"], w=[("ps", 6, "A")])
                            self.tt("dve", amt, ab[:, 0:128], self.btri, ALU.mult, r=[("ps", 6, "A")], w=[("am", t % 2)])
                        for hf, (qx, kx) in enumerate(((qhA, ktA), (qhB, ktB))):
                            c = 2 * t + hf
                            if pas == 1:
                                self.mm(ob[:, 0:128], qx[:, t, :], Sb[si], hf == 0, False,
                                        r=["qhA", "qhB", ("Sb", si)], w=[("ps", 7)])
                            sbk = self.ps[6]
                            self.mm(sbk[:, 128:256], kx[:, t, :], iv, True, True, r=["ktA", "ktB", ("itok", t)],
                                    w=[("ps", 6, "S")])
                            self.tt("dve", tmpS, S[si], sbk[:, 128:256], ALU.add, r=[("S", si), ("ps", 6, "S")], w=["tmpS"])
                            sn = 1 - si
                            self.act(S[sn], tmpS, AF.Identity, r=["tmpS", "dec"], w=[("S", sn)], scale=dec[:, c:c + 1])
                            self.cp("pool", Sb[sn], S[sn], r=[("S", sn)], w=[("Sb", sn)])
                            si = sn
                        if pas == 1:
                            self.mm(ob[:, 0:128], amt, iv, False, True, r=[("am", t % 2), ("itok", t)], w=[("ps", 7)])
                            self.act(sq, ob[:, 0:128], AF.Square, r=[("ps", 7)], w=["sq"])
                            self.p.op("dve", lambda e: e.tensor_reduce(out=ss, in_=sq, axis=AX.X, op=ALU.add), r=["sq"], w=["ss"])
                            self.act(rs, ss, AF.Sqrt, r=["ss"], w=["rs"], scale=1.0 / 128, bias=self.epsrms)
                            self.p.op("dve", lambda e: e.reciprocal(out=rs, in_=rs), r=["rs"], w=["rs"])
                            self.stt("dve", tmpo, ob[:, 0:128], rs, gn, ALU.mult, ALU.mult, r=[("ps", 7), "rs", "gn"], w=["tmpo"])
                            self.tt("pool", oh[:, t, :], tmpo, sgtok[:, t, hh * 128:(hh + 1) * 128], ALU.mult,
                                    r=["tmpo", ("sgtok", t)], w=["oh"])
                    if pas == 0:
                        self.dma("sp", self.s_own[h * 16:(h + 1) * 16, :].rearrange("r (a v) -> (r a) v", v=128), S[si], r=[("S", si)], w=[("s_own", h)])
                    else:
                        self.dma("sp", self.o_dram[h * T:(h + 1) * T, :].rearrange("(t p) d -> p t d", p=128), oh,
                                 r=["oh"], w=[("o_dram", h)])
            if pas == 0:
                self.p.barrier()
                for g_ in range(2 if not NOCC else 0):
                    self.p.cc(lambda e, g_=g_: e.collective_compute(
                        "AllGather", ALU.bypass, replica_groups=PAIRS, ins=[self.s_own[g_ * 128:(g_ + 1) * 128, :]],
                        outs=[self.s_all[g_ * 256:(g_ + 1) * 256, :]]), r=[], w=[("s_all", g_)])
                self.p.barrier()
        self.p.barrier()
        ar.release(m0)

    def layer_norm_tile(self, u, stats, mv, rstd, nmr, g_bc, b_bc, out, ukey, okey, pfx):
        self.p.op("dve", lambda e: e.bn_aggr(out=mv, in_=stats.rearrange("p a b -> p (a b)")),
                  r=[pfx + "stats"], w=[pfx + "mv"])
        self.act(rstd, mv[:, 1:2], AF.Sqrt, r=[pfx + "mv"], w=[pfx + "rstd"], bias=self.epsln)
        self.p.op("dve", lambda e: e.reciprocal(out=rstd, in_=rstd), r=[pfx + "rstd"], w=[pfx + "rstd"])
        self.stt("dve", nmr, mv[:, 0:1], -1.0, rstd, ALU.mult, ALU.mult, r=[pfx + "mv", pfx + "rstd"], w=[pfx + "nmr"])
        self.act(u, u, AF.Identity, r=[ukey, pfx + "rstd", pfx + "nmr"], w=[ukey], scale=rstd, bias=nmr)
        self.tt("pool", u, u, g_bc, ALU.mult, r=[ukey, "lnp"], w=[ukey])
        self.tt("dve", out, u, b_bc, ALU.add, r=[ukey, "lnp"], w=[okey])

    def wo_ln_router(self, x_src, w_o, l):
        ar = self.ar
        self.m_keep = ar.mark()
        wo = ar.alloc([DC, D], BF16)
        for cg in range(4):
            self.load_w(wo[:, :, cg * 512:(cg + 1) * 512], w_o[:, cg * 512:(cg + 1) * 512], ("wo", cg))
        g_bc = ar.alloc([D], F32)
        b_bc = ar.alloc([D], F32)
        self.dma("sp", g_bc, self.ln_mix_g[l].partition_broadcast(128), r=[], w=["lnp"])
        self.dma("sp", b_bc, self.ln_mix_b[l].partition_broadcast(128), r=[], w=["lnp"])
        wr = ar.alloc([DC, 36], F32)
        with self.nc.allow_non_contiguous_dma(reason="tiny router weights"):
            self.dma("sp", wr[:, :, 0:4], self.moe_w_rg[l].rearrange("(c p) n -> p c n", p=128), r=[], w=["wr"])
            for g in range(4):
                self.dma("sp", wr[:, :, 4 + g * 8:12 + g * 8],
                         self.moe_w_re[l][g].rearrange("(c p) n -> p c n", p=128), r=[], w=["wr"])
        brg = ar.alloc([4], F32)
        bre = ar.alloc([32], F32)
        self.dma("sp", brg, self.moe_b_rg[l].partition_broadcast(128), r=[], w=["brg"])
        self.dma("sp", bre, self.moe_b_re[l].partition_broadcast(128), r=[], w=["bre"])
        macc = ar.alloc([32], F32)
        self.memset("dve", macc, 0.0, w=["macc"])
        ot = [ar.alloc([NH, 128], BF16) for _ in range(2)]
        oT = [ar.alloc([DC, 128], BF16) for _ in range(2)]
        xt = [ar.alloc([D], F32) for _ in range(2)]
        uu = [ar.alloc([D], F32) for _ in range(2)]
        x1t = [ar.alloc([D], F32) for _ in range(2)]
        x1b = [ar.alloc([D], BF16) for _ in range(2)]
        x1T = ar.alloc([DC, 128], F32)
        stats = ar.alloc([4, 6], F32)
        mv = ar.alloc([2], F32)
        rstd = ar.alloc([1], F32)
        nmr = ar.alloc([1], F32)
        lgs = ar.alloc([4], F32)
        les = ar.alloc([32], F32)
        gmax = ar.alloc([1], F32)
        ngmax = ar.alloc([1], F32)
        eg = ar.alloc([4], F32)
        gsum = ar.alloc([1], F32)
        gw = ar.alloc([1], F32)
        goh = ar.alloc([4], F32)
        gpen = ar.alloc([4], F32)
        lm = ar.alloc([4, 8], F32)
        t8 = ar.alloc([8], F32)
        oh1 = ar.alloc([32], F32)
        oh2 = ar.alloc([32], F32)
        mt = ar.alloc([32], F32)
        dd = ar.alloc([1], F32)
        ee = ar.alloc([1], F32)
        den = ar.alloc([1], F32)
        rr = ar.alloc([1], F32)
        posf = ar.alloc([32], F32)
        tmp = ar.alloc([32], F32)
        sf = ar.alloc([2], F32)
        o_view = self.o_dram.rearrange("(h t p) d -> t p h d", h=NH, p=128)
        for t in range(NT):
            s = t % 2
            k = lambda n: (n, s)
            self.dma("sp", ot[s], o_view[t], r=[], w=[k("ot")])
            self.dma("sp", xt[s], x_src[t * 128:(t + 1) * 128, :], r=[], w=[k("xt")])
            for half in range(2):
                bank = self.psb(half)
                for c8 in range(8):
                    c = half * 8 + c8
                    self.tr(bank[:, c8 * 128:(c8 + 1) * 128], ot[s][:, c, :], self.identb, r=[k("ot")], w=[("ps", half)])
                self.cp("dve" if half == 0 else "act", oT[s][:, half * 8:(half + 1) * 8, :],
                        bank.rearrange("p (a b) -> p a b", b=128), r=[("ps", half)], w=[k("oT")])
            for cg in range(4):
                b = 2 + cg
                bank = self.ps[b]
                for c in range(DC):
                    self.mm(bank[:, :], oT[s][:, c, :], wo[:, c, cg * 512:(cg + 1) * 512], c == 0, c == DC - 1,
                            r=[k("oT"), ("wo", cg)], w=[("ps", b)])
                self.stt("dve", uu[s][:, cg * 512:(cg + 1) * 512], xt[s][:, cg * 512:(cg + 1) * 512], ALPHA, bank[:, :],
                         ALU.mult, ALU.add, r=[k("xt"), ("ps", b)], w=[k("uu")])
                self.p.op("dve", lambda e, s=s, cg=cg: e.bn_stats(out=stats[:, cg, :], in_=uu[s][:, cg * 512:(cg + 1) * 512]),
                          r=[k("uu")], w=["Astats"])
            self.layer_norm_tile(uu[s], stats, mv, rstd, nmr, g_bc, b_bc, x1t[s], k("uu"), k("x1t"), "A")
            self.dma("sp", self.x1[t * 128:(t + 1) * 128, :], x1t[s], r=[k("x1t")], w=[("x1d", t)])
            self.cp("act", x1b[s], x1t[s], r=[k("x1t")], w=[k("x1b")])
            for c4 in range(4):
                b = 6 + c4 % 2
                bank = self.ps[b]
                for cc in range(4):
                    c = c4 * 4 + cc
                    self.tr(bank[:, cc * 128:(cc + 1) * 128], x1t[s][:, c * 128:(c + 1) * 128], self.identf,
                            r=[k("x1t")], w=[("ps", b)])
                self.cp("act" if c4 % 2 else "dve", x1T[:, c4 * 4:(c4 + 1) * 4, :],
                        bank[:, :].rearrange("p (a b) -> p a b", b=128), r=[("ps", b)], w=[("x1T", c4)])
            lb = self.ps[0]
            for c in range(DC):
                self.mm(lb[:, 0:36], x1T[:, c, :], wr[:, c, :], c == 0, c == DC - 1,
                        r=[("x1T", c // 4), "wr"], w=[("ps", 0)])
            self.tt("dve", lgs, lb[:, 0:4], brg, ALU.add, r=[("ps", 0), "brg"], w=["lgs"])
            self.tt("dve", les, lb[:, 4:36], bre, ALU.add, r=[("ps", 0), "bre"], w=["les"])
            self.p.op("dve", lambda e: e.tensor_reduce(out=gmax, in_=lgs, axis=AX.X, op=ALU.max), r=["lgs"], w=["gmax"])
            self.ts("dve", ngmax, gmax, -1.0, None, ALU.mult, r=["gmax"], w=["ngmax"])
            self.act(eg, lgs, AF.Exp, r=["lgs", "ngmax"], w=["eg"], bias=ngmax)
            self.p.op("dve", lambda e: e.tensor_reduce(out=gsum, in_=eg, axis=AX.X, op=ALU.add), r=["eg"], w=["gsum"])
            self.p.op("dve", lambda e: e.reciprocal(out=gw, in_=gsum), r=["gsum"], w=["gw"])
            self.ts("dve", goh, lgs, gmax, None, ALU.is_ge, r=["lgs", "gmax"], w=["goh"])
            self.ts("dve", gpen, goh, 1.0, 1.0e30, ALU.subtract, r=["goh"], w=["gpen"], op1=ALU.mult)
            self.tt("dve", lm, les.rearrange("p (a b) -> p a b", b=8), gpen.unsqueeze(2).to_broadcast([128, 4, 8]),
                    ALU.add, r=["les", "gpen"], w=["lm"])
            lm2 = lm.rearrange("p a b -> p (a b)")
            self.p.op("dve", lambda e: e.max(out=t8, in_=lm2), r=["lm"], w=["t8"])
            self.ts("dve", oh1, lm2, t8[:, 0:1], None, ALU.is_equal, r=["lm", "t8"], w=["oh1"])
            self.ts("dve", oh2, lm2, t8[:, 1:2], None, ALU.is_equal, r=["lm", "t8"], w=["oh2"])
            self.tt("dve", mt, oh1, oh2, ALU.add, r=["oh1", "oh2"], w=["mt"])
            self.tt("dve", dd, t8[:, 1:2], t8[:, 0:1], ALU.subtract, r=["t8"], w=["dd"])
            self.act(ee, dd, AF.Exp, r=["dd"], w=["ee"])
            self.ts("dve", den, ee, 1.0, None, ALU.add, r=["ee"], w=["den"])
            self.p.op("dve", lambda e: e.reciprocal(out=rr, in_=den), r=["den"], w=["rr"])
            self.tt("dve", self.wts[:, t, 0:1], rr, gw, ALU.mult, r=["rr", "gw"], w=[("wts", t)])
            self.tt("dve", self.wts[:, t, 1:2], self.wts[:, t, 0:1], ee, ALU.mult, r=[("wts", t), "ee"], w=[("wts", t)])
            pb = self.ps[1]
            self.mm(pb[:, 0:32], self.tstrict, mt, True, False, r=["mt"], w=[("ps", 1)])
            self.mm(pb[:, 0:32], self.ones, macc, False, True, r=["macc"], w=[("ps", 1)])
            self.tt("dve", posf, pb[:, 0:32], self.eoff, ALU.add, r=[("ps", 1)], w=["posf"])
            self.tt("dve", macc, macc, mt, ALU.add, r=["mt", "macc", ("ps", 1)], w=["macc"])
            for kk, ohk in enumerate((oh1, oh2)):
                self.tt("dve", tmp, ohk, posf, ALU.mult, r=["oh1", "oh2", "posf"], w=["tmp"])
                self.p.op("dve", lambda e, kk=kk: e.tensor_reduce(out=sf[:, kk:kk + 1], in_=tmp, axis=AX.X, op=ALU.add),
                          r=["tmp"], w=["sf"])
            self.ts("dve", sf, sf, 0.0, float(NSLOT + 127), ALU.max, r=["sf"], w=["sf"], op1=ALU.min)
            self.cp("dve", self.slots[:, t, :], sf, r=["sf"], w=[("slots", t)])
            for kk in range(2):
                self.p.dma("pool", lambda e, s=s, t=t, kk=kk: e.indirect_dma_start(
                    out=self.xg, out_offset=bass.IndirectOffsetOnAxis(ap=self.slots[:, t, kk:kk + 1], axis=0),
                    in_=x1b[s], in_offset=None), r=[k("x1b"), ("slots", t)], w=[("xg", t, kk)])
        self.p.barrier()
        ar.release(self.m_keep)

    def experts(self, l):
        ar = self.ar
        m0 = ar.mark()
        wg = [ar.alloc([DC, 512], BF16) for _ in range(2)]
        wu = [ar.alloc([DC, 512], BF16) for _ in range(2)]
        wd = [ar.alloc([4, D], BF16) for _ in range(2)]
        xe = [ar.alloc([2, D], BF16) for _ in range(2)]
        xeT = [ar.alloc([DC, 256], BF16) for _ in range(2)]
        sg = [ar.alloc([256], F32) for _ in range(2)]
        hT = [ar.alloc([4, 256], BF16) for _ in range(2)]
        yo = [ar.alloc([D], F32) for _ in range(2)]
        nb = 0
        ny = 0
        for e_ in range(NE):
            s = e_ % 2
            k = lambda n: (n, s)
            self.load_w(wg[s], self.moe_w_gate[l][e_], k("wg"))
            self.load_w(wu[s], self.moe_w_up[l][e_], k("wu"))
            self.dma("pool", wd[s], self.moe_w_down[l][e_].rearrange("(c p) n -> p c n", p=128), r=[], w=[k("wd")])
            self.dma("sp", xe[s], self.xg[e_ * CAP:(e_ + 1) * CAP, :].rearrange("(a p) n -> p a n", p=128),
                     r=[], w=[k("xe")])
            for c4 in range(4):
                b = c4 % 2
                bank = self.psb(b).rearrange("p (a b) -> p a b", b=256)
                for cc in range(4):
                    c = c4 * 4 + cc
                    for rt in range(2):
                        self.tr(bank[:, cc, rt * 128:(rt + 1) * 128], xe[s][:, rt, c * 128:(c + 1) * 128], self.identb,
                                r=[k("xe")], w=[("ps", b)])
                self.cp("act" if c4 % 2 else "dve", xeT[s][:, c4 * 4:(c4 + 1) * 4, :], bank,
                        r=[("ps", b)], w=[k("xeT")])
            for fc in range(4):
                b = 2 + nb % 2
                nb += 1
                bank = self.ps[b]
                for c in range(DC):
                    self.mm(bank[:, 0:256], wg[s][:, c, fc * 128:(fc + 1) * 128], xeT[s][:, c, :], c == 0, c == DC - 1,
                            r=[k("wg"), k("xeT")], w=[("ps", b)])
                for c in range(DC):
                    self.mm(bank[:, 256:512], wu[s][:, c, fc * 128:(fc + 1) * 128], xeT[s][:, c, :], c == 0, c == DC - 1,
                            r=[k("wu"), k("xeT")], w=[("ps", b)])
                ss = nb % 2
                self.act(sg[ss], bank[:, 0:256], AF.Silu, r=[("ps", b)], w=[("sg", ss)])
                self.tt("dve", hT[s][:, fc, :], sg[ss], bank[:, 256:512], ALU.mult, r=[("sg", ss), ("ps", b)],
                        w=[k("hT")])
            for rt in range(2):
                ys = ny % 2
                ny += 1
                for cg in range(4):
                    b = 4 + cg
                    bank = self.ps[b]
                    for fc in range(4):
                        self.mm(bank[:, :], hT[s][:, fc, rt * 128:(rt + 1) * 128], wd[s][:, fc, cg * 512:(cg + 1) * 512],
                                fc == 0, fc == 3, r=[k("hT"), k("wd")], w=[("ps", b)])
                    self.cp("act" if cg % 2 else "dve", yo[ys][:, cg * 512:(cg + 1) * 512], bank[:, :],
                            r=[("ps", b)], w=[("yo", ys, cg)])
                r0 = e_ * CAP + rt * 128
                self.dma("sp", self.yg[r0:r0 + 128, :], yo[ys], r=[("yo", ys, cg) for cg in range(4)], w=[("yg", e_, rt)])
        self.p.barrier()
        ar.release(m0)

    def combine_ln(self, l, dst):
        ar = self.ar
        m0 = ar.mark()
        g_bc = ar.alloc([D], F32)
        b_bc = ar.alloc([D], F32)
        self.dma("sp", g_bc, self.ln_ffn_g[l].partition_broadcast(128), r=[], w=["lnp"])
        self.dma("sp", b_bc, self.ln_ffn_b[l].partition_broadcast(128), r=[], w=["lnp"])
        y1 = [ar.alloc([D], F32) for _ in range(2)]
        y2 = [ar.alloc([D], F32) for _ in range(2)]
        xt = [ar.alloc([D], F32) for _ in range(2)]
        ot = [ar.alloc([D], F32) for _ in range(2)]
        stats = ar.alloc([4, 6], F32)
        mv = ar.alloc([2], F32)
        rstd = ar.alloc([1], F32)
        nmr = ar.alloc([1], F32)
        for t in range(NT):
            s = t % 2
            k = lambda n: (n, s)
            for kk, yb in enumerate((y1[s], y2[s])):
                self.p.dma("pool", lambda e, t=t, kk=kk, yb=yb: e.indirect_dma_start(
                    out=yb, out_offset=None, in_=self.yg,
                    in_offset=bass.IndirectOffsetOnAxis(ap=self.slots[:, t, kk:kk + 1], axis=0)),
                    r=[], w=[k("y%d" % kk)])
            self.dma("sp", xt[s], self.x1[t * 128:(t + 1) * 128, :], r=[], w=[k("xt")])
            self.ts("pool", xt[s], xt[s], ALPHA, None, ALU.mult, r=[k("xt")], w=[k("xt")])
            self.stt("dve", xt[s], y1[s], self.wts[:, t, 0:1], xt[s], ALU.mult, ALU.add, r=[k("y0"), k("xt")], w=[k("xt")])
            self.stt("dve", xt[s], y2[s], self.wts[:, t, 1:2], xt[s], ALU.mult, ALU.add, r=[k("y1"), k("xt")], w=[k("xt")])
            for cg in range(4):
                self.p.op("dve", lambda e, s=s, cg=cg: e.bn_stats(out=stats[:, cg, :], in_=xt[s][:, cg * 512:(cg + 1) * 512]),
                          r=[k("xt")], w=["Bstats"])
            self.layer_norm_tile(xt[s], stats, mv, rstd, nmr, g_bc, b_bc, ot[s], k("xt"), k("ot"), "B")
            self.dma("sp", dst[t * 128:(t + 1) * 128, :], ot[s], r=[k("ot")], w=[("dst", t)])
        self.p.barrier()
        ar.release(m0)


def make_consts(core):
    h = core % 2
    c = {}
    c["c_identb"] = np.eye(128, dtype=np.float32).astype(ml_dtypes.bfloat16)
    c["c_identf"] = np.eye(128, dtype=np.float32)
    i = np.arange(128)
    c["c_trile"] = (i[:, None] <= i[None, :]).astype(np.float32).astype(ml_dtypes.bfloat16)
    c["c_btri"] = ((i[:, None] <= i[None, :]) & ((i[:, None] // 64) == (i[None, :] // 64))).astype(np.float32).astype(
        ml_dtypes.bfloat16)
    c["c_tstrict"] = (i[:, None] < i[None, :]).astype(np.float32)
    c["c_ones"] = np.ones((128, 128), np.float32)
    c["c_eoff"] = np.broadcast_to((np.arange(32) * CAP).astype(np.float32), (128, 32)).copy()
    gb = np.zeros((16, 16), np.float32)
    for qt in range(16):
        for p in range(16):
            valid = (h == 1) if p < 8 else ((p - 8) < qt // 2)
            gb[qt, p] = 0.0 if valid else NEG
    c["c_gbias"] = np.broadcast_to(gb.reshape(1, 256), (128, 256)).copy()
    c["c_hflag"] = np.full((128, 1), float(h), np.float32)
    pm = np.zeros((128, 2), np.float32)
    pm[:64, 0] = 1.0
    pm[64:, 1] = 1.0
    c["c_pmask"] = pm
    return c


_CACHE = {}


def kernel(**inputs):
    if "nc" not in _CACHE:
        _CACHE["nc"] = K().build()
    nc = _CACHE["nc"]
    x = np.asarray(inputs["x"], np.float32)
    in_maps = []
    BIG = ("moba_w_in", "moba_w_o", "hgrn_w_in", "hgrn_w_o", "moe_w_gate", "moe_w_up", "moe_w_down")
    for c in range(8):
        b, h = c // 2, c % 2
        m = {}
        for k, v in inputs.items():
            if k == "x":
                continue
            v = np.asarray(v, np.float32)
            if k in BIG and WMODE == "fake":
                continue
            if k in BIG and WMODE == "repl":
                m[k] = np.ascontiguousarray(v.reshape(-1, v.shape[-1]))
                continue
            if k in BIG:
                pieces = v.shape[0]
                C = v.shape[-1]
                v2 = v.reshape(pieces, 8, -1, C)[:, c]
                m[k] = np.ascontiguousarray(v2.reshape(-1, C))
            else:
                m[k] = np.ascontiguousarray(v)
        m["moe_b_re"] = m["moe_b_re"].reshape(DEPTH, 32)
        m["x"] = np.ascontiguousarray(x[b, h * T:(h + 1) * T, :])
        m.update(make_consts(c))
        in_maps.append(m)
    res = run_bass_kernel_spmd(nc, in_maps, core_ids=list(range(8)))
    out = np.zeros((4, 4096, D), np.float32)
    for c in range(8):
        b, h = c // 2, c % 2
        out[b, h * T:(h + 1) * T, :] = res.results[c]["out"]
    return out
```

```python
from contextlib import ExitStack
import numpy as np
import ml_dtypes
import concourse.bass as bass
import concourse.mybir as mybir
from concourse.bass_utils import run_bass_kernel_spmd

F32 = mybir.dt.float32
BF16 = mybir.dt.bfloat16
I32 = mybir.dt.int32
AF = mybir.ActivationFunctionType
ALU = mybir.AluOpType
AX = mybir.AxisListType

ENGS = ["pe", "act", "dve", "pool", "sp"]
NDMA = 6

D = 2048
T = 2048
NT = 16
DC = 16
NH = 16
DEPTH = 4
ALPHA = (2 * DEPTH) ** 0.25
LN_EPS = 1e-5
RMS_EPS = 1e-6
CAP = 256
NE = 32
NSLOT = NE * CAP
SCALE = 128 ** -0.5
NEG = -1.0e30
PAIRS = [[0, 1], [2, 3], [4, 5], [6, 7]]
HSTOP = 0
NOCC = 0
WMODE = "repl"


class Prog:
    def __init__(self, nc, stack):
        self.nc = nc
        self.ops = {e: [] for e in ENGS}
        self.sem = {e: stack.enter_context(nc.semaphore("s_" + e)) for e in ENGS}
        self.cnt = {e: 0 for e in ENGS}
        self.dsem = {e: [stack.enter_context(nc.semaphore("d_%s%d" % (e, i))) for i in range(NDMA)]
                     for e in ("sp", "act", "pool")}
        self.dcnt = {e: 0 for e in ("sp", "act", "pool")}
        self.ccsem = stack.enter_context(nc.semaphore("s_cc"))
        self.cccnt = 0
        self.sems = {("cc",): self.ccsem}
        for e in ENGS:
            self.sems[("c", e)] = self.sem[e]
        for e in self.dsem:
            for i, s in enumerate(self.dsem[e]):
                self.sems[("d", e, i)] = s
        self.seen = {e: {} for e in ENGS}
        self.last_w = {}
        self.readers = {}

    def _need(self, eng, tok, waits):
        if tok is None:
            return
        sid, val, teng = tok
        if eng == "pe" and teng == "pe" and sid[0] == "c":
            return
        if self.seen[eng].get(sid, 0) >= val:
            return
        self.seen[eng][sid] = val
        waits.append((sid, val))

    def _deps(self, eng, r, w, waits):
        for k in r:
            self._need(eng, self.last_w.get(k), waits)
        for k in w:
            self._need(eng, self.last_w.get(k), waits)
            for t in self.readers.get(k, ()):
                self._need(eng, t, waits)

    def _commit(self, tok, r, w):
        for k in r:
            self.readers.setdefault(k, []).append(tok)
        for k in w:
            self.last_w[k] = tok
            self.readers[k] = []

    def op(self, eng, fn, r=(), w=()):
        waits = []
        self._deps(eng, r, w, waits)
        self.cnt[eng] += 1
        tok = (("c", eng), self.cnt[eng], eng)
        self.ops[eng].append((waits, fn, ("c", eng), 1))
        self._commit(tok, r, w)

    def dma(self, q, fn, r=(), w=()):
        waits = []
        self._deps(q, r, w, waits)
        i = self.dcnt[q]
        self.dcnt[q] += 1
        slot = i % NDMA
        sid = ("d", q, slot)
        prev = (i // NDMA) * 16
        if prev > 0:
            self._need(q, (sid, prev, "dma"), waits)
        tok = (sid, prev + 16, "dma")
        self.ops[q].append((waits, fn, sid, 16))
        self._commit(tok, r, w)

    def cc(self, fn, r=(), w=()):
        waits = []
        self._deps("pool", r, w, waits)
        self.cccnt += 1
        tok = (("cc",), self.cccnt, "dma")
        self.ops["pool"].append((waits, fn, ("cc",), None))
        self._commit(tok, r, w)

    def wait_all(self, eng):
        waits = []
        for e in ENGS:
            if self.cnt[e]:
                self._need(eng, (("c", e), self.cnt[e], "x"), waits)
        for q in self.dsem:
            n = self.dcnt[q]
            for slot in range(NDMA):
                k = (n - slot + NDMA - 1) // NDMA if n > slot else 0
                if k:
                    self._need(eng, (("d", q, slot), 16 * k, "dma"), waits)
        if self.cccnt:
            self._need(eng, (("cc",), self.cccnt, "dma"), waits)
        self.ops[eng].append((waits, None, None, 0))

    def barrier(self):
        for e in ENGS:
            self.wait_all(e)
        self.last_w = {}
        self.readers = {}

    def emit(self):
        nc = self.nc
        with nc.Block() as block:
            def run(e, name):
                for waits, fn, sid, amt in self.ops[name]:
                    for (ws, wv) in waits:
                        e.wait_ge(self.sems[ws], wv)
                    if fn is not None:
                        ins = fn(e)
                        if amt is None:
                            ins.then_inc(self.sems[sid])
                        else:
                            ins.then_inc(self.sems[sid], amt)

            @block.tensor
            def _(e):
                run(e, "pe")

            @block.scalar
            def _(e):
                run(e, "act")

            @block.vector
            def _(e):
                run(e, "dve")

            @block.gpsimd
            def _(e):
                run(e, "pool")

            @block.sync
            def _(e):
                run(e, "sp")


class Arena:
    def __init__(self, nc, stack, nbytes):
        self.t = stack.enter_context(nc.sbuf_tensor("arena", [128, nbytes // 2], BF16))
        self.n = nbytes
        self.off = 0

    def alloc(self, free_shape, dt):
        esz = 4 if dt in (F32, I32) else 2
        n = int(np.prod(free_shape)) * esz
        self.off = (self.off + 63) // 64 * 64
        assert self.off + n <= self.n, ("SBUF arena overflow", self.off, n, self.n)
        a = self.t[:, self.off // 2:(self.off + n) // 2]
        if esz == 4:
            a = a.bitcast(dt)
        self.off += n
        if len(free_shape) == 2:
            a = a.rearrange("p (a b) -> p a b", b=free_shape[1])
        elif len(free_shape) == 3:
            a = a.rearrange("p (a b c) -> p a b c", b=free_shape[1], c=free_shape[2])
        return a

    def mark(self):
        return self.off

    def release(self, m):
        self.off = m


class K:
    def __init__(self, n_layers=DEPTH, dbg=False, only=None):
        self.only = only
        self.n_layers = n_layers
        self.dbg = dbg
        self.nc = bass.Bass("TRN2", target_bir_lowering=False)

    def mm(self, out, lhsT, rhs, start, stop, r, w):
        self.p.op("pe", lambda e: e.matmul(out, lhsT=lhsT, rhs=rhs, start=start, stop=stop), r=r, w=w)

    def tr(self, out, in_, ident, r, w):
        self.p.op("pe", lambda e: e.transpose(out, in_, ident), r=r, w=w)

    def act(self, out, in_, func, r, w, **kw):
        self.p.op("act", lambda e: e.activation(out=out, in_=in_, func=func, **kw), r=r, w=w)

    def cp(self, eng, out, in_, r, w):
        if eng == "act":
            self.p.op("act", lambda e: e.copy(out=out, in_=in_), r=r, w=w)
        else:
            self.p.op(eng, lambda e: e.tensor_copy(out=out, in_=in_), r=r, w=w)

    def tt(self, eng, out, in0, in1, op, r, w):
        self.p.op(eng, lambda e: e.tensor_tensor(out=out, in0=in0, in1=in1, op=op), r=r, w=w)

    def ts(self, eng, out, in0, s1, s2, op0, r, w, op1=None):
        if op1 is None:
            self.p.op(eng, lambda e: e.tensor_scalar(out=out, in0=in0, scalar1=s1, scalar2=None, op0=op0), r=r, w=w)
        else:
            self.p.op(eng, lambda e: e.tensor_scalar(out=out, in0=in0, scalar1=s1, scalar2=s2, op0=op0, op1=op1),
                      r=r, w=w)

    def stt(self, eng, out, in0, scalar, in1, op0, op1, r, w):
        self.p.op(eng, lambda e: e.scalar_tensor_tensor(out=out, in0=in0, scalar=scalar, in1=in1, op0=op0, op1=op1),
                  r=r, w=w)

    def dma(self, q, out, in_, r, w):
        self.p.dma(q, lambda e: e.dma_start(out=out, in_=in_), r=r, w=w)

    def memset(self, eng, ap, val, w):
        self.p.op(eng, lambda e: e.memset(ap, val), w=w)

    def build_layer(self, kind, j, mixer_only=False):
        nc = self.nc
        di = lambda name, shape, dt=F32: nc.dram_tensor(name, shape, dt, kind="ExternalInput").ap()
        self.x_in = di("x", [T, D])
        self.gath = []
        if kind == "moba":
            w = di("w_in", [D, 3 * D])
            self.moba_w_in = [w, w]
        else:
            w = di("w_in", [D, 4 * D])
            self.hgrn_w_in = [w, w]
            gnm = di("g_norm", [128])
            self.hgrn_g_norm = [gnm, gnm]
            self.hgrn_lb_raw = di("lb_raw", [2, D])
        if not mixer_only:
            wo = di("w_o", [D, D])
            rep = lambda a: [a] * DEPTH
            self.ln_mix_g = rep(di("ln_mix_g", [D]))
            self.ln_mix_b = rep(di("ln_mix_b", [D]))
            self.moe_w_rg = rep(di("moe_w_rg", [D, 4]))
            self.moe_b_rg = rep(di("moe_b_rg", [4]))
            self.moe_w_re = rep(di("moe_w_re", [4, D, 8]))
            self.moe_b_re = rep(di("moe_b_re", [32]))
            self.moe_w_gate = rep(di("moe_w_gate", [NE, D, 512]))
            self.moe_w_up = rep(di("moe_w_up", [NE, D, 512]))
            self.moe_w_down = rep(di("moe_w_down", [NE, 512, D]))
            self.ln_ffn_g = rep(di("ln_ffn_g", [D]))
            self.ln_ffn_b = rep(di("ln_ffn_b", [D]))
        self.c_identb = di("c_identb", [128, 128], BF16)
        self.c_identf = di("c_identf", [128, 128])
        self.c_trile = di("c_trile", [128, 128], BF16)
        self.c_btri = di("c_btri", [128, 128], BF16)
        self.c_tstrict = di("c_tstrict", [128, 128])
        self.c_ones = di("c_ones", [128, 128])
        self.c_eoff = di("c_eoff", [128, 32])
        self.c_gbias = di("c_gbias", [128, 256])
        self.c_hflag = di("c_hflag", [128, 1])
        self.c_pmask = di("c_pmask", [128, 2])
        dt = lambda name, shape, dtp=F32: nc.dram_tensor(name, shape, dtp).ap()
        if mixer_only:
            self.o_dram = nc.dram_tensor("o_out", [NH * T, 128], BF16, kind="ExternalOutput").ap()
        else:
            self.out = nc.dram_tensor("out", [T, D], F32, kind="ExternalOutput").ap()
            self.o_dram = dt("o_dram", [NH * T, 128], BF16)
        self.xa = dt("xa", [T, D])
        self.x1 = dt("x1", [T, D])
        self.qT = dt("qT", [NH * 128, T], BF16)
        self.kT_own = dt("kT_own", [NH * 128, T], BF16)
        self.kT_all = dt("kT_all", [2 * NH * 128, T], BF16)
        self.v_own = dt("v_own", [NH * T, 128], BF16)
        self.v_all = dt("v_all", [2 * NH * T, 128], BF16)
        self.xg = dt("xg", [NSLOT + 128, D], BF16)
        self.yg = dt("yg", [NSLOT + 128, D])
        self.s_own = dt("s_own", [256, 1024])
        self.s_all = dt("s_all", [512, 1024])
        with ExitStack() as st:
            self.st = st
            self.p = Prog(nc, st)
            self.ar = Arena(nc, st, 207 * 1024)
            self.ps = [st.enter_context(nc.psum_tensor("ps%d" % i, [128, 512], F32)) for i in range(8)]
            self.consts()
            if kind == "moba":
                self.moba_proj(self.x_in, j)
                self.moba_attn()
            else:
                self.hgrn_layer(self.x_in, j)
            if not mixer_only:
                self.wo_ln_router(self.x_in, wo, 0)
                self.experts(0)
                self.combine_ln(0, self.out)
            self.p.barrier()
            self.p.emit()
        return nc

    def build(self):
        nc = self.nc
        di = lambda name, shape, dt=F32: nc.dram_tensor(name, shape, dt, kind="ExternalInput").ap()
        self.x_in = di("x", [T, D])
        self.gath = []
        def big(name, shape):
            pieces = shape[0]
            if WMODE == "fake":
                per_ = int(np.prod(shape[1:-1]))
                return [nc.dram_tensor(name + "_f%d" % i, [per_, shape[-1]], F32).ap() for i in range(pieces)]
            if WMODE == "repl":
                per_ = int(np.prod(shape[1:-1]))
                full_ = di(name, [pieces * per_, shape[-1]])
                return [full_[i * per_:(i + 1) * per_, :] for i in range(pieces)]
            R = int(np.prod(shape[:-1])); C = shape[-1]
            per = R // pieces
            ext = di(name, [R // 8, C])
            stg = [nc.dram_tensor(name + "_stg%d" % i, [per // 8, C], F32).ap() for i in range(pieces)]
            full = [nc.dram_tensor(name + "_full%d" % i, [per, C], F32).ap() for i in range(pieces)]
            self.gath.append((ext, stg, full, pieces, per))
            return full
        self.moba_w_in = big("moba_w_in", [2, D, 3 * D])
        self.moba_w_o = big("moba_w_o", [2, D, D])
        self.hgrn_w_in = big("hgrn_w_in", [2, D, 4 * D])
        self.hgrn_g_norm = di("hgrn_g_norm", [2, 128])
        self.hgrn_lb_raw = di("hgrn_lb_raw", [2, D])
        self.hgrn_w_o = big("hgrn_w_o", [2, D, D])
        self.ln_mix_g = di("ln_mix_g", [DEPTH, D])
        self.ln_mix_b = di("ln_mix_b", [DEPTH, D])
        self.moe_w_rg = di("moe_w_rg", [DEPTH, D, 4])
        self.moe_b_rg = di("moe_b_rg", [DEPTH, 4])
        self.moe_w_re = di("moe_w_re", [DEPTH, 4, D, 8])
        self.moe_b_re = di("moe_b_re", [DEPTH, 32])
        self.moe_w_gate = [f.rearrange("(e r) c -> e r c", e=NE) for f in big("moe_w_gate", [DEPTH, NE, D, 512])]
        self.moe_w_up = [f.rearrange("(e r) c -> e r c", e=NE) for f in big("moe_w_up", [DEPTH, NE, D, 512])]
        self.moe_w_down = [f.rearrange("(e r) c -> e r c", e=NE) for f in big("moe_w_down", [DEPTH, NE, 512, D])]
        self.ln_ffn_g = di("ln_ffn_g", [DEPTH, D])
        self.ln_ffn_b = di("ln_ffn_b", [DEPTH, D])
        self.c_identb = di("c_identb", [128, 128], BF16)
        self.c_identf = di("c_identf", [128, 128])
        self.c_trile = di("c_trile", [128, 128], BF16)
        self.c_btri = di("c_btri", [128, 128], BF16)
        self.c_tstrict = di("c_tstrict", [128, 128])
        self.c_ones = di("c_ones", [128, 128])
        self.c_eoff = di("c_eoff", [128, 32])
        self.c_gbias = di("c_gbias", [128, 256])
        self.c_hflag = di("c_hflag", [128, 1])
        self.c_pmask = di("c_pmask", [128, 2])
        self.out = nc.dram_tensor("out", [T, D], F32, kind="ExternalOutput").ap()
        dt = lambda name, shape, dtp=F32: nc.dram_tensor(name, shape, dtp).ap()
        self.xa = dt("xa", [T, D])
        self.x1 = dt("x1", [T, D])
        self.qT = dt("qT", [NH * 128, T], BF16)
        self.kT_own = dt("kT_own", [NH * 128, T], BF16)
        self.kT_all = dt("kT_all", [2 * NH * 128, T], BF16)
        self.v_own = dt("v_own", [NH * T, 128], BF16)
        self.v_all = dt("v_all", [2 * NH * T, 128], BF16)
        self.o_dram = dt("o_dram", [NH * T, 128], BF16)
        self.xg = dt("xg", [NSLOT + 128, D], BF16)
        self.yg = dt("yg", [NSLOT + 128, D])
        self.s_own = dt("s_own", [256, 1024])
        self.s_all = dt("s_all", [512, 1024])
        self.dbgs = {}
        with ExitStack() as st:
            self.st = st
            self.p = Prog(nc, st)
            self.ar = Arena(nc, st, 207 * 1024)
            self.ps = [st.enter_context(nc.psum_tensor("ps%d" % i, [128, 512], F32)) for i in range(8)]
            self.consts()
            self.gather_weights()
            x_src = self.x_in
            for l in range(self.n_layers):
                j = l // 2
                last = (l == self.n_layers - 1)
                on = lambda ph: self.only is None or ph in self.only
                if l % 2 == 0:
                    if on("proj"):
                        self.moba_proj(x_src, j)
                    if on("attn"):
                        self.moba_attn()
                    w_o = self.moba_w_o[j]
                else:
                    self.hgrn_layer(x_src, j)
                    w_o = self.hgrn_w_o[j]
                if on("wo"):
                    self.wo_ln_router(x_src, w_o, l)
                if on("exp"):
                    self.experts(l)
                if on("comb"):
                    self.combine_ln(l, self.out if last else self.xa)
                else:
                    self.dma("sp", self.out[0:128, :], self.x_in[0:128, :], r=[], w=["o"])
                x_src = self.xa
            self.p.barrier()
            self.p.emit()
        return nc

    def consts(self):
        ar = self.ar
        self.identb = ar.alloc([128], BF16)
        self.identf = ar.alloc([128], F32)
        self.trile = ar.alloc([128], BF16)
        self.btri = ar.alloc([128], BF16)
        self.tstrict = ar.alloc([128], F32)
        self.ones = ar.alloc([128], F32)
        self.eoff = ar.alloc([32], F32)
        self.gbias = ar.alloc([256], F32)
        self.hflag = ar.alloc([1], F32)
        self.pmask = ar.alloc([2], F32)
        self.slots = ar.alloc([NT, 2], I32)
        self.wts = ar.alloc([NT, 2], F32)
        self.epsln = ar.alloc([1], F32)
        self.epsrms = ar.alloc([1], F32)
        self.memset("dve", self.epsln, LN_EPS, w=["const"])
        self.memset("dve", self.epsrms, RMS_EPS, w=["const"])
        for sb, dr in ((self.identb, self.c_identb), (self.identf, self.c_identf), (self.trile, self.c_trile),
                       (self.btri, self.c_btri), (self.tstrict, self.c_tstrict), (self.ones, self.c_ones),
                       (self.eoff, self.c_eoff), (self.gbias, self.c_gbias), (self.hflag, self.c_hflag),
                       (self.pmask, self.c_pmask)):
            self.dma("sp", sb, dr, r=[], w=["const"])
        self.p.barrier()

    def gather_weights(self):
        for gi, (ext, stg, full, pieces, per) in enumerate(self.gath):
            ps_ = per // 8
            for i in range(pieces):
                self.dma("sp", stg[i], ext[i * ps_:(i + 1) * ps_, :], r=[], w=[("wstg", gi, i)])
        self.p.barrier()
        for gi, (ext, stg, full, pieces, per) in enumerate(self.gath):
            for i in range(pieces):
                self.p.cc(lambda e, a=stg[i], b=full[i]: e.collective_compute(
                    "AllGather", ALU.bypass, replica_groups=[list(range(8))],
                    ins=[a], outs=[b]), r=[], w=[("wfull", gi, i)])
        self.p.barrier()

    def psb(self, i):
        return self.ps[i][:, :].bitcast(BF16)

    def build_xT(self, x_src, xT):
        ar = self.ar
        m = ar.mark()
        xin = [ar.alloc([D], F32) for _ in range(2)]
        xb = [ar.alloc([D], BF16) for _ in range(2)]
        for t in range(NT):
            s = t % 2
            self.dma("sp", xin[s], x_src[t * 128:(t + 1) * 128, :], r=[], w=[("xin", s)])
            self.cp("act", xb[s], xin[s], r=[("xin", s)], w=[("xb", s)])
            for half in range(2):
                bank = self.psb(half)
                for c8 in range(8):
                    c = half * 8 + c8
                    self.tr(bank[:, c8 * 128:(c8 + 1) * 128], xb[s][:, c * 128:(c + 1) * 128], self.identb,
                            r=[("xb", s)], w=[("ps", half)])
                self.cp("dve" if half == 0 else "pool" if False else "dve",
                        xT[:, half * 8:(half + 1) * 8, t * 128:(t + 1) * 128],
                        bank.rearrange("p (a b) -> p a b", b=128), r=[("ps", half)], w=[("xT", t)])
        self.p.barrier()
        ar.release(m)

    def load_w(self, wt, src, key):
        self.dma("pool", wt, src.rearrange("(c p) n -> p c n", p=128), r=[], w=[key])

    def moba_proj(self, x_src, j):
        ar = self.ar
        m0 = ar.mark()
        xT = ar.alloc([DC, T], BF16)
        self.build_xT(x_src, xT)
        wb = [ar.alloc([DC, 512], BF16) for _ in range(2)]
        stg = [ar.alloc([T], BF16) for _ in range(2)]
        vst = [ar.alloc([512], BF16) for _ in range(2)]
        w_in = self.moba_w_in[j]
        xT_keys = [("xT", t) for t in range(NT)]
        nb = 0
        wi = 0
        for cg in range(8):
            wt = wb[wi % 2]
            wkey = ("wb", wi % 2)
            wi += 1
            self.load_w(wt, w_in[:, cg * 512:(cg + 1) * 512], wkey)
            for hh in range(4):
                head = (cg % 4) * 4 + hh
                s = hh % 2
                for tb in range(4):
                    b = 2 + nb % 4
                    nb += 1
                    bank = self.ps[b]
                    for c in range(DC):
                        self.mm(bank[:, :], wt[:, c, hh * 128:(hh + 1) * 128], xT[:, c, tb * 512:(tb + 1) * 512],
                                c == 0, c == DC - 1, r=[wkey] + xT_keys[tb * 4:(tb + 1) * 4], w=[("ps", b)])
                    self.cp("act" if nb % 2 else "dve", stg[s][:, tb * 512:(tb + 1) * 512], bank[:, :],
                            r=[("ps", b)], w=[("stg", s, tb)])
                dst = self.qT if cg < 4 else self.kT_own
                self.dma("sp", dst[head * 128:(head + 1) * 128, :], stg[s],
                         r=[("stg", s, tb) for tb in range(4)], w=[("qk", cg < 4, head)])
        v_view = self.v_own.rearrange("(h t p) d -> t p h d", h=NH, p=128)
        for cg in range(4):
            wt = wb[wi % 2]
            wkey = ("wb", wi % 2)
            wi += 1
            self.load_w(wt, w_in[:, 2 * D + cg * 512:2 * D + (cg + 1) * 512], wkey)
            for t in range(NT):
                b = 2 + nb % 4
                nb += 1
                bank = self.ps[b]
                for c in range(DC):
                    self.mm(bank[:, :], xT[:, c, t * 128:(t + 1) * 128], wt[:, c, :], c == 0, c == DC - 1,
                            r=[wkey, ("xT", t)], w=[("ps", b)])
                s = t % 2
                self.cp("act" if nb % 2 else "dve", vst[s], bank[:, :], r=[("ps", b)], w=[("vst", s)])
                self.dma("sp", v_view[t][:, cg * 4:(cg + 1) * 4, :], vst[s].rearrange("p (h d) -> p h d", d=128),
                         r=[("vst", s)], w=[("v", cg, t)])
        self.p.barrier()
        if self.only is not None and "cc" not in self.only:
            ar.release(m0)
            return
        for h in range(NH):
            self.p.cc(lambda e, h=h: e.collective_compute(
                "AllGather", ALU.bypass, replica_groups=PAIRS, ins=[self.kT_own[h * 128:(h + 1) * 128, :]],
                outs=[self.kT_all[h * 256:(h + 1) * 256, :]]), r=[], w=[("kT_all", h)])
            self.p.cc(lambda e, h=h: e.collective_compute(
                "AllGather", ALU.bypass, replica_groups=PAIRS, ins=[self.v_own[h * T:(h + 1) * T, :]],
                outs=[self.v_all[h * 2 * T:(h + 1) * 2 * T, :]]), r=[], w=[("v_all", h)])
        self.p.barrier()
        ar.release(m0)

    def moba_attn(self):
        ar = self.ar
        m0 = ar.mark()
        qh = [ar.alloc([T], BF16) for _ in range(2)]
        kf = [ar.alloc([T], BF16) for _ in range(2)]
        ko = [ar.alloc([T], BF16) for _ in range(2)]
        vf = [ar.alloc([NT, 130], BF16) for _ in range(2)]
        vo = [ar.alloc([NT, 130], BF16) for _ in range(2)]
        oh = [ar.alloc([NT, 128], BF16) for _ in range(2)]
        km = ar.alloc([16], F32)
        kmhi = ar.alloc([16], BF16)
        kmhf = ar.alloc([16], F32)
        kmlo = ar.alloc([16], BF16)
        gm = ar.alloc([16, 16], F32)
        top8 = ar.alloc([16, 8], F32)
        thr = ar.alloc([16], F32)
        sel = ar.alloc([16, 16], F32)
        pT = [ar.alloc([2, 256], BF16) for _ in range(2)]
        acc = [ar.alloc([2, 130], F32) for _ in range(2)]
        rec = ar.alloc([2], F32)
        for s in range(2):
            self.memset("pool", vf[s][:, :, 128:130], 1.0, w=[("vf", s)])
            self.memset("pool", vo[s][:, :, 128:130], 1.0, w=[("vo", s)])
        u = 0

        def head_loads(h2):
            s2 = h2 % 2
            self.dma("sp", qh[s2], self.qT[h2 * 128:(h2 + 1) * 128, :], r=[], w=[("qh", s2)])
            self.dma("sp", kf[s2], self.kT_all[h2 * 256:h2 * 256 + 128, :], r=[], w=[("kf", s2)])
            self.dma("sp", ko[s2], self.kT_own[h2 * 128:(h2 + 1) * 128, :], r=[], w=[("ko", s2)])
            self.dma("sp", vf[s2][:, :, 0:128],
                     self.v_all[h2 * 2 * T:h2 * 2 * T + T, :].rearrange("(t p) d -> p t d", p=128), r=[], w=[("vf", s2)])
            self.dma("sp", vo[s2][:, :, 0:128],
                     self.v_own[h2 * T:(h2 + 1) * T, :].rearrange("(t p) d -> p t d", p=128), r=[], w=[("vo", s2)])
        head_loads(0)
        for h in range(NH):
            s = h % 2
            hk = lambda n: (n, s)
            if h + 1 < NH:
                head_loads(h + 1)
            self.p.op("dve", lambda e, s=s: e.tensor_reduce(out=km[:, 0:8], in_=kf[s].rearrange("p (b k) -> p b k", k=256),
                                                          axis=AX.X, op=ALU.add), r=[hk("kf")], w=["km"])
            self.p.op("dve", lambda e, s=s: e.tensor_reduce(out=km[:, 8:16], in_=ko[s].rearrange("p (b k) -> p b k", k=256),
                                                          axis=AX.X, op=ALU.add), r=[hk("ko")], w=["km"])
            self.ts("dve", km, km, 1.0 / 256, None, ALU.mult, r=["km"], w=["km"])
            self.cp("dve", kmhi, km, r=["km"], w=["kmhi"])
            self.cp("dve", kmhf, kmhi, r=["kmhi"], w=["kmhf"])
            self.tt("dve", kmlo, km, kmhf, ALU.subtract, r=["km", "kmhf"], w=["kmlo"])
            gb = self.ps[6]
            for qt in range(NT):
                self.mm(gb[:, qt * 16:(qt + 1) * 16], qh[s][:, qt * 128:(qt + 1) * 128], kmhi, True, False,
                        r=[hk("qh"), "kmhi"], w=[("ps", 6)])
                self.mm(gb[:, qt * 16:(qt + 1) * 16], qh[s][:, qt * 128:(qt + 1) * 128], kmlo, False, True,
                        r=[hk("qh"), "kmlo"], w=[("ps", 6)])
            gm2 = gm.rearrange("p a b -> p (a b)")
            self.tt("dve", gm2, gb[:, 0:256], self.gbias, ALU.add, r=[("ps", 6)], w=["gm"])
            for qt in range(NT):
                self.p.op("dve", lambda e, qt=qt: e.max(out=top8[:, qt, :], in_=gm[:, qt, :]), r=["gm"], w=["top8"])
            self.ts("dve", thr, top8[:, :, 2], -1.0e29, None, ALU.max, r=["top8"], w=["thr"])
            self.tt("dve", sel, gm, thr.unsqueeze(2).to_broadcast([128, 16, 16]), ALU.is_ge,
                    r=["gm", "thr"], w=["sel"])
            for jb in range(8):
                kbl = [("f", pp) for pp in range(8)] + [("o", pp) for pp in range(jb)] + [("own", jb)]
                a = acc[jb % 2]
                akey = ("acc", jb % 2)
                qs = qh[s][:, jb * 256:(jb + 1) * 256]
                for bi, (kind, pp) in enumerate(kbl):
                    ksrc = kf[s] if kind == "f" else ko[s]
                    vsrc = vf[s] if kind == "f" else vo[s]
                    kkey = hk("kf") if kind == "f" else hk("ko")
                    vkey = hk("vf") if kind == "f" else hk("vo")
                    own = kind == "own"
                    sb_i = u % 2
                    ob_i = 2 + u % 2
                    pt = pT[u % 2]
                    pkey = ("pT", u % 2)
                    u += 1
                    sbank = self.ps[sb_i][:, :].rearrange("p (a b) -> p a b", b=256)
                    obank = self.ps[ob_i][:, :].rearrange("p (a b) -> p a b", b=256)
                    self.mm(sbank[:, 0, :], ksrc[:, pp * 256:pp * 256 + 128], qs, True, True,
                            r=[kkey, hk("qh")], w=[("ps", sb_i)])
                    if own:
                        self.mm(sbank[:, 1, 128:256], ksrc[:, pp * 256 + 128:pp * 256 + 256], qs[:, 128:256], True, True,
                                r=[kkey, hk("qh")], w=[("ps", sb_i)])
                        self.act(pt[:, 0, :], sbank[:, 0, :], AF.Exp, r=[("ps", sb_i)], w=[pkey], scale=SCALE)
                        self.act(pt[:, 1, 128:256], sbank[:, 1, 128:256], AF.Exp, r=[("ps", sb_i)], w=[pkey], scale=SCALE)
                        self.tt("pool", pt[:, 0, 0:128], pt[:, 0, 0:128], self.trile, ALU.mult, r=[pkey], w=[pkey])
                        self.tt("pool", pt[:, 1, 128:256], pt[:, 1, 128:256], self.trile, ALU.mult, r=[pkey], w=[pkey])
                    else:
                        self.mm(sbank[:, 1, :], ksrc[:, pp * 256 + 128:pp * 256 + 256], qs, True, True,
                                r=[kkey, hk("qh")], w=[("ps", sb_i)])
                        self.act(pt, sbank, AF.Exp, r=[("ps", sb_i)], w=[pkey], scale=SCALE)
                    for qi in range(2):
                        kts = [0] if (own and qi == 0) else [0, 1]
                        for n, kt in enumerate(kts):
                            self.mm(obank[:, qi, 0:129], pt[:, kt, qi * 128:(qi + 1) * 128], vsrc[:, pp * 2 + kt, 0:129],
                                    n == 0, n == len(kts) - 1, r=[pkey, vkey], w=[("ps", ob_i)])
                    for qi in range(2):
                        qt = 2 * jb + qi
                        if own:
                            self.tt("dve", a[:, qi, 0:129], obank[:, qi, 0:129], a[:, qi, 0:129], ALU.add,
                                    r=[("ps", ob_i), akey], w=[akey])
                        else:
                            idx = pp if kind == "f" else 8 + pp
                            if bi == 0:
                                self.ts("dve", a[:, qi, 0:129], obank[:, qi, 0:129], sel[:, qt, idx:idx + 1], None,
                                        ALU.mult, r=[("ps", ob_i), "sel", akey], w=[akey])
                            else:
                                self.stt("dve", a[:, qi, 0:129], obank[:, qi, 0:129], sel[:, qt, idx:idx + 1],
                                         a[:, qi, 0:129], ALU.mult, ALU.add, r=[("ps", ob_i), "sel", akey], w=[akey])
                self.p.op("dve", lambda e, a=a: e.reciprocal(out=rec, in_=a[:, :, 128]), r=[akey], w=["rec"])
                for qi in range(2):
                    self.ts("dve", oh[s][:, 2 * jb + qi, :], a[:, qi, 0:128], rec[:, qi:qi + 1], None, ALU.mult,
                            r=[akey, "rec"], w=[hk("oh")])
            self.dma("sp", self.o_dram[h * T:(h + 1) * T, :].rearrange("(t p) d -> p t d", p=128), oh[s],
                     r=[hk("oh")], w=[("o_dram", h)])
        self.p.barrier()
        ar.release(m0)

    def hgrn_layer(self, x_src, j):
        ar = self.ar
        m0 = ar.mark()
        xT = ar.alloc([DC, T], BF16)
        self.build_xT(x_src, xT)
        w_in = self.hgrn_w_in[j]
        xT_keys = [("xT", t) for t in range(NT)]
        lbt = ar.alloc([16], F32)
        oml = ar.alloc([16], F32)
        if j == 0:
            self.memset("dve", lbt, 0.0, w=["lbt"])
            self.memset("dve", oml, 1.0, w=["oml"])
        else:
            r0 = ar.alloc([16], F32)
            r1 = ar.alloc([16], F32)
            self.p.dma("sp", lambda e: e.dma_start(out=r0, in_=self.hgrn_lb_raw[0].rearrange("(h p) -> p h", p=128),
                                                   allow_slow_non_contiguous=True), r=[], w=["r0"])
            self.p.dma("sp", lambda e: e.dma_start(out=r1, in_=self.hgrn_lb_raw[1].rearrange("(h p) -> p h", p=128),
                                                   allow_slow_non_contiguous=True), r=[], w=["r1"])
            self.tt("dve", r1, r1, r0, ALU.subtract, r=["r0", "r1"], w=["r1"])
            self.act(lbt, r1, AF.Sigmoid, r=["r1"], w=["lbt"])
            self.ts("dve", oml, lbt, -1.0, 1.0, ALU.mult, r=["lbt"], w=["oml"], op1=ALU.add)
        gn = ar.alloc([128], F32)
        self.dma("sp", gn, self.hgrn_g_norm[j].partition_broadcast(128), r=[], w=["gn"])
        onesb = ar.alloc([T], BF16)
        self.memset("dve", onesb, 1.0, w=["onesb"])
        wb = [ar.alloc([DC, 512], BF16) for _ in range(2)]
        itok = ar.alloc([NT, 512], BF16)
        sgtok = ar.alloc([NT, 512], BF16)
        A = ar.alloc([T], F32)
        B = ar.alloc([T], F32)
        Cc = ar.alloc([T], F32)
        E = ar.alloc([T], F32)
        qhat = ar.alloc([T], BF16)
        qhA = ar.alloc([NT, 128], BF16)
        qhB = ar.alloc([NT, 128], BF16)
        khat = ar.alloc([T], BF16)
        ktA = ar.alloc([NT, 128], BF16)
        ktB = ar.alloc([NT, 128], BF16)
        oh = ar.alloc([NT, 128], BF16)
        basec = ar.alloc([32], F32)
        dec = ar.alloc([32], F32)
        S = [ar.alloc([128], F32) for _ in range(2)]
        Sb = [ar.alloc([128], BF16) for _ in range(2)]
        tmpS = ar.alloc([128], F32)
        am = [ar.alloc([128], BF16) for _ in range(2)]
        sq = ar.alloc([128], F32)
        ss = ar.alloc([1], F32)
        rs = ar.alloc([1], F32)
        tmpo = ar.alloc([128], F32)
        nb = 0
        for pas in range(2):
            for hg in range(4):
                for which, dstb in ((2, itok), (3, sgtok)):
                    if which == 3 and pas == 0:
                        continue
                    wt = wb[0]
                    self.load_w(wt, w_in[:, which * D + hg * 512:which * D + (hg + 1) * 512], ("wb", 0))
                    for t in range(NT):
                        b = 2 + nb % 4
                        nb += 1
                        bank = self.ps[b]
                        for c in range(DC):
                            self.mm(bank[:, :], xT[:, c, t * 128:(t + 1) * 128], wt[:, c, :], c == 0, c == DC - 1,
                                    r=[("wb", 0), ("xT", t)], w=[("ps", b)])
                        if which == 2:
                            self.cp("dve", dstb[:, t, :], bank[:, :], r=[("ps", b)], w=[("itok", t)])
                        else:
                            self.act(dstb[:, t, :], bank[:, :], AF.Silu, r=[("ps", b)], w=[("sgtok", t)])
                if pas == 1:
                    self.load_w(wb[0], w_in[:, hg * 512:(hg + 1) * 512], ("wb", 0))
                self.load_w(wb[1], w_in[:, D + hg * 512:D + (hg + 1) * 512], ("wb", 1))
                for hh in range(4):
                    h = hg * 4 + hh
                    for tb in range(4):
                        for wi_, (dst, fn) in enumerate(((A, AF.Silu), (B, AF.Sigmoid))):
                            if pas == 0 and wi_ == 0:
                                continue
                            b = 2 + nb % 4
                            nb += 1
                            bank = self.ps[b]
                            for c in range(DC):
                                self.mm(bank[:, :], wb[wi_][:, c, hh * 128:(hh + 1) * 128], xT[:, c, tb * 512:(tb + 1) * 512],
                                        c == 0, c == DC - 1, r=[("wb", wi_)] + xT_keys[tb * 4:(tb + 1) * 4], w=[("ps", b)])
                            self.act(dst[:, tb * 512:(tb + 1) * 512], bank[:, :], fn, r=[("ps", b)],
                                     w=[("A" if wi_ == 0 else "B", tb)])
                    if HSTOP == 1:
                        continue
                    Ak = [("A", tb) for tb in range(4)]
                    Bk = [("B", tb) for tb in range(4)]
                    self.ts("dve", B, B, oml[:, h:h + 1], lbt[:, h:h + 1], ALU.mult, r=Bk + ["oml", "lbt"], w=["Bf"], op1=ALU.add)
                    self.ts("dve", Cc, B, -1.0, 1.0, ALU.mult, r=["Bf"], w=["Cc"], op1=ALU.add)
                    self.act(B, B, AF.Ln, r=["Bf", "Cc"], w=["Bf"])
                    self.p.op("dve", lambda e: e.tensor_tensor_scan(out=E, data0=onesb, data1=B, initial=0.0,
                                                                    op0=ALU.mult, op1=ALU.add), r=["Bf", "onesb"], w=["E"])
                    E3 = E.rearrange("p (c t) -> p c t", t=64)
                    self.memset("dve", basec[:, 0:1], 0.0, w=["basec"])
                    self.cp("dve", basec[:, 1:32], E3[:, 0:31, 63], r=["E"], w=["basec"])
                    self.tt("dve", E3, E3, basec.unsqueeze(2).to_broadcast([128, 32, 64]), ALU.subtract,
                            r=["E", "basec"], w=["E"])
                    self.act(dec, E3[:, :, 63], AF.Exp, r=["E"], w=["dec"])
                    if pas == 1:
                        self.act(B, E, AF.Exp, r=["E", "Bf"], w=["Bf"])
                        self.tt("dve", qhat, A, B, ALU.mult, r=Ak + ["Bf"], w=["qhat"])
                    self.act(B, E, AF.Exp, r=["E", "Bf", "qhat"], w=["Bf"], scale=-1.0)
                    self.tt("dve", khat, Cc, B, ALU.mult, r=["Cc", "Bf"], w=["khat"])
                    if pas == 1:
                        q3 = qhat.rearrange("p (t k) -> p t k", k=128)
                        self.cp("pool", qhA, q3, r=["qhat"], w=["qhA"])
                        self.cp("pool", qhB, q3, r=["qhat"], w=["qhB"])
                        self.memset("pool", qhA[:, :, 64:128], 0.0, w=["qhA"])
                        self.memset("pool", qhB[:, :, 0:64], 0.0, w=["qhB"])
                    for half in range(2):
                        bank = self.psb(half)
                        for c8 in range(8):
                            t = half * 8 + c8
                            self.tr(bank[:, c8 * 128:(c8 + 1) * 128], khat[:, t * 128:(t + 1) * 128], self.identb,
                                    r=["khat"], w=[("ps", half)])
                        src = bank.rearrange("p (a b) -> p a b", b=128)
                        self.ts("dve", ktA[:, half * 8:(half + 1) * 8, :], src, self.pmask[:, 0:1], None, ALU.mult,
                                r=[("ps", half)], w=["ktA"])
                        self.ts("dve", ktB[:, half * 8:(half + 1) * 8, :], src, self.pmask[:, 1:2], None, ALU.mult,
                                r=[("ps", half)], w=["ktB"])
                    if HSTOP == 2:
                        continue
                    si = 0
                    if pas == 0:
                        self.memset("dve", S[0], 0.0, w=[("S", 0)])
                    else:
                        self.dma("sp", tmpS, self.s_all[(h // 8) * 256 + (h % 8) * 16:(h // 8) * 256 + (h % 8) * 16 + 16, :].rearrange("r (a v) -> (r a) v", v=128), r=[], w=["tmpS"])
                        self.ts("dve", S[0], tmpS, self.hflag[:, 0:1], None, ALU.mult, r=["tmpS"], w=[("S", 0)])
                    self.cp("pool", Sb[0], S[0], r=[("S", 0)], w=[("Sb", 0)])
                    for t in range(NT):
                        iv = itok[:, t, hh * 128:(hh + 1) * 128]
                        ob = self.ps[7]
                        if pas == 1:
                            ab = self.ps[6]
                            amt = am[t % 2]
                            self.mm(ab[:, 0:128], khat[:, t * 128:(t + 1) * 128], qhat[:, t * 128:(t + 1) * 128], True, True,
                                    r=["khat", "qhat"], w=[("ps", 6, "A")])
                            self.tt("dve", amt, ab[:, 0:128], self.btri, ALU.mult, r=[("ps", 6, "A")], w=[("am", t % 2)])
                        for hf, (qx, kx) in enumerate(((qhA, ktA), (qhB, ktB))):
                            c = 2 * t + hf
                            if pas == 1:
                                self.mm(ob[:, 0:128], qx[:, t, :], Sb[si], hf == 0, False,
                                        r=["qhA", "qhB", ("Sb", si)], w=[("ps", 7)])
                            sbk = self.ps[6]
                            self.mm(sbk[:, 128:256], kx[:, t, :], iv, True, True, r=["ktA", "ktB", ("itok", t)],
                                    w=[("ps", 6, "S")])
                            self.tt("dve", tmpS, S[si], sbk[:, 128:256], ALU.add, r=[("S", si), ("ps", 6, "S")], w=["tmpS"])
                            sn = 1 - si
                            self.act(S[sn], tmpS, AF.Identity, r=["tmpS", "dec"], w=[("S", sn)], scale=dec[:, c:c + 1])
                            self.cp("pool", Sb[sn], S[sn], r=[("S", sn)], w=[("Sb", sn)])
                            si = sn
                        if pas == 1:
                            self.mm(ob[:, 0:128], amt, iv, False, True, r=[("am", t % 2), ("itok", t)], w=[("ps", 7)])
                            self.act(sq, ob[:, 0:128], AF.Square, r=[("ps", 7)], w=["sq"])
                            self.p.op("dve", lambda e: e.tensor_reduce(out=ss, in_=sq, axis=AX.X, op=ALU.add), r=["sq"], w=["ss"])
                            self.act(rs, ss, AF.Sqrt, r=["ss"], w=["rs"], scale=1.0 / 128, bias=self.epsrms)
                            self.p.op("dve", lambda e: e.reciprocal(out=rs, in_=rs), r=["rs"], w=["rs"])
                            self.stt("dve", tmpo, ob[:, 0:128], rs, gn, ALU.mult, ALU.mult, r=[("ps", 7), "rs", "gn"], w=["tmpo"])
                            self.tt("pool", oh[:, t, :], tmpo, sgtok[:, t, hh * 128:(hh + 1) * 128], ALU.mult,
                                    r=["tmpo", ("sgtok", t)], w=["oh"])
                    if pas == 0:
                        self.dma("sp", self.s_own[h * 16:(h + 1) * 16, :].rearrange("r (a v) -> (r a) v", v=128), S[si], r=[("S", si)], w=[("s_own", h)])
                    else:
                        self.dma("sp", self.o_dram[h * T:(h + 1) * T, :].rearrange("(t p) d -> p t d", p=128), oh,
                                 r=["oh"], w=[("o_dram", h)])
            if pas == 0:
                self.p.barrier()
                for g_ in range(2 if not NOCC else 0):
                    self.p.cc(lambda e, g_=g_: e.collective_compute(
                        "AllGather", ALU.bypass, replica_groups=PAIRS, ins=[self.s_own[g_ * 128:(g_ + 1) * 128, :]],
                        outs=[self.s_all[g_ * 256:(g_ + 1) * 256, :]]), r=[], w=[("s_all", g_)])
                self.p.barrier()
        self.p.barrier()
        ar.release(m0)

    def layer_norm_tile(self, u, stats, mv, rstd, nmr, g_bc, b_bc, out, ukey, okey, pfx):
        self.p.op("dve", lambda e: e.bn_aggr(out=mv, in_=stats.rearrange("p a b -> p (a b)")),
                  r=[pfx + "stats"], w=[pfx + "mv"])
        self.act(rstd, mv[:, 1:2], AF.Sqrt, r=[pfx + "mv"], w=[pfx + "rstd"], bias=self.epsln)
        self.p.op("dve", lambda e: e.reciprocal(out=rstd, in_=rstd), r=[pfx + "rstd"], w=[pfx + "rstd"])
        self.stt("dve", nmr, mv[:, 0:1], -1.0, rstd, ALU.mult, ALU.mult, r=[pfx + "mv", pfx + "rstd"], w=[pfx + "nmr"])
        self.act(u, u, AF.Identity, r=[ukey, pfx + "rstd", pfx + "nmr"], w=[ukey], scale=rstd, bias=nmr)
        self.tt("pool", u, u, g_bc, ALU.mult, r=[ukey, "lnp"], w=[ukey])
        self.tt("dve", out, u, b_bc, ALU.add, r=[ukey, "lnp"], w=[okey])

    def wo_ln_router(self, x_src, w_o, l):
        ar = self.ar
        self.m_keep = ar.mark()
        wo = ar.alloc([DC, D], BF16)
        for cg in range(4):
            self.load_w(wo[:, :, cg * 512:(cg + 1) * 512], w_o[:, cg * 512:(cg + 1) * 512], ("wo", cg))
        g_bc = ar.alloc([D], F32)
        b_bc = ar.alloc([D], F32)
        self.dma("sp", g_bc, self.ln_mix_g[l].partition_broadcast(128), r=[], w=["lnp"])
        self.dma("sp", b_bc, self.ln_mix_b[l].partition_broadcast(128), r=[], w=["lnp"])
        wr = ar.alloc([DC, 36], F32)
        with self.nc.allow_non_contiguous_dma(reason="tiny router weights"):
            self.dma("sp", wr[:, :, 0:4], self.moe_w_rg[l].rearrange("(c p) n -> p c n", p=128), r=[], w=["wr"])
            for g in range(4):
                self.dma("sp", wr[:, :, 4 + g * 8:12 + g * 8],
                         self.moe_w_re[l][g].rearrange("(c p) n -> p c n", p=128), r=[], w=["wr"])
        brg = ar.alloc([4], F32)
        bre = ar.alloc([32], F32)
        self.dma("sp", brg, self.moe_b_rg[l].partition_broadcast(128), r=[], w=["brg"])
        self.dma("sp", bre, self.moe_b_re[l].partition_broadcast(128), r=[], w=["bre"])
        macc = ar.alloc([32], F32)
        self.memset("dve", macc, 0.0, w=["macc"])
        ot = [ar.alloc([NH, 128], BF16) for _ in range(2)]
        oT = [ar.alloc([DC, 128], BF16) for _ in range(2)]
        xt = [ar.alloc([D], F32) for _ in range(2)]
        uu = [ar.alloc([D], F32) for _ in range(2)]
        x1t = [ar.alloc([D], F32) for _ in range(2)]
        x1b = [ar.alloc([D], BF16) for _ in range(2)]
        x1T = ar.alloc([DC, 128], F32)
        stats = ar.alloc([4, 6], F32)
        mv = ar.alloc([2], F32)
        rstd = ar.alloc([1], F32)
        nmr = ar.alloc([1], F32)
        lgs = ar.alloc([4], F32)
        les = ar.alloc([32], F32)
        gmax = ar.alloc([1], F32)
        ngmax = ar.alloc([1], F32)
        eg = ar.alloc([4], F32)
        gsum = ar.alloc([1], F32)
        gw = ar.alloc([1], F32)
        goh = ar.alloc([4], F32)
        gpen = ar.alloc([4], F32)
        lm = ar.alloc([4, 8], F32)
        t8 = ar.alloc([8], F32)
        oh1 = ar.alloc([32], F32)
        oh2 = ar.alloc([32], F32)
        mt = ar.alloc([32], F32)
        dd = ar.alloc([1], F32)
        ee = ar.alloc([1], F32)
        den = ar.alloc([1], F32)
        rr = ar.alloc([1], F32)
        posf = ar.alloc([32], F32)
        tmp = ar.alloc([32], F32)
        sf = ar.alloc([2], F32)
        o_view = self.o_dram.rearrange("(h t p) d -> t p h d", h=NH, p=128)
        def tile_loads(t2):
            s2 = t2 % 2
            self.dma("sp", ot[s2], o_view[t2], r=[], w=[("ot", s2)])
            self.dma("sp", xt[s2], x_src[t2 * 128:(t2 + 1) * 128, :], r=[], w=[("xt", s2)])
        tile_loads(0)
        for t in range(NT):
            s = t % 2
            k = lambda n: (n, s)
            if t + 1 < NT:
                tile_loads(t + 1)
            for half in range(2):
                bank = self.psb(half)
                for c8 in range(8):
                    c = half * 8 + c8
                    self.tr(bank[:, c8 * 128:(c8 + 1) * 128], ot[s][:, c, :], self.identb, r=[k("ot")], w=[("ps", half)])
                self.cp("dve" if half == 0 else "act", oT[s][:, half * 8:(half + 1) * 8, :],
                        bank.rearrange("p (a b) -> p a b", b=128), r=[("ps", half)], w=[k("oT")])
            for cg in range(4):
                b = 2 + cg
                bank = self.ps[b]
                for c in range(DC):
                    self.mm(bank[:, :], oT[s][:, c, :], wo[:, c, cg * 512:(cg + 1) * 512], c == 0, c == DC - 1,
                            r=[k("oT"), ("wo", cg)], w=[("ps", b)])
                self.stt("dve", uu[s][:, cg * 512:(cg + 1) * 512], xt[s][:, cg * 512:(cg + 1) * 512], ALPHA, bank[:, :],
                         ALU.mult, ALU.add, r=[k("xt"), ("ps", b)], w=[k("uu")])
                self.p.op("dve", lambda e, s=s, cg=cg: e.bn_stats(out=stats[:, cg, :], in_=uu[s][:, cg * 512:(cg + 1) * 512]),
                          r=[k("uu")], w=["Astats"])
            self.layer_norm_tile(uu[s], stats, mv, rstd, nmr, g_bc, b_bc, x1t[s], k("uu"), k("x1t"), "A")
            self.dma("sp", self.x1[t * 128:(t + 1) * 128, :], x1t[s], r=[k("x1t")], w=[("x1d", t)])
            self.cp("act", x1b[s], x1t[s], r=[k("x1t")], w=[k("x1b")])
            for c4 in range(4):
                b = 6 + c4 % 2
                bank = self.ps[b]
                for cc in range(4):
                    c = c4 * 4 + cc
                    self.tr(bank[:, cc * 128:(cc + 1) * 128], x1t[s][:, c * 128:(c + 1) * 128], self.identf,
                            r=[k("x1t")], w=[("ps", b)])
                self.cp("act" if c4 % 2 else "dve", x1T[:, c4 * 4:(c4 + 1) * 4, :],
                        bank[:, :].rearrange("p (a b) -> p a b", b=128), r=[("ps", b)], w=[("x1T", c4)])
            lb = self.ps[0]
            for c in range(DC):
                self.mm(lb[:, 0:36], x1T[:, c, :], wr[:, c, :], c == 0, c == DC - 1,
                        r=[("x1T", c // 4), "wr"], w=[("ps", 0)])
            self.tt("dve", lgs, lb[:, 0:4], brg, ALU.add, r=[("ps", 0), "brg"], w=["lgs"])
            self.tt("dve", les, lb[:, 4:36], bre, ALU.add, r=[("ps", 0), "bre"], w=["les"])
            self.p.op("dve", lambda e: e.tensor_reduce(out=gmax, in_=lgs, axis=AX.X, op=ALU.max), r=["lgs"], w=["gmax"])
            self.ts("dve", ngmax, gmax, -1.0, None, ALU.mult, r=["gmax"], w=["ngmax"])
            self.act(eg, lgs, AF.Exp, r=["lgs", "ngmax"], w=["eg"], bias=ngmax)
            self.p.op("dve", lambda e: e.tensor_reduce(out=gsum, in_=eg, axis=AX.X, op=ALU.add), r=["eg"], w=["gsum"])
            self.p.op("dve", lambda e: e.reciprocal(out=gw, in_=gsum), r=["gsum"], w=["gw"])
            self.ts("dve", goh, lgs, gmax, None, ALU.is_ge, r=["lgs", "gmax"], w=["goh"])
            self.ts("dve", gpen, goh, 1.0, 1.0e30, ALU.subtract, r=["goh"], w=["gpen"], op1=ALU.mult)
            self.tt("dve", lm, les.rearrange("p (a b) -> p a b", b=8), gpen.unsqueeze(2).to_broadcast([128, 4, 8]),
                    ALU.add, r=["les", "gpen"], w=["lm"])
            lm2 = lm.rearrange("p a b -> p (a b)")
            self.p.op("dve", lambda e: e.max(out=t8, in_=lm2), r=["lm"], w=["t8"])
            self.ts("dve", oh1, lm2, t8[:, 0:1], None, ALU.is_equal, r=["lm", "t8"], w=["oh1"])
            self.ts("dve", oh2, lm2, t8[:, 1:2], None, ALU.is_equal, r=["lm", "t8"], w=["oh2"])
            self.tt("dve", mt, oh1, oh2, ALU.add, r=["oh1", "oh2"], w=["mt"])
            self.tt("dve", dd, t8[:, 1:2], t8[:, 0:1], ALU.subtract, r=["t8"], w=["dd"])
            self.act(ee, dd, AF.Exp, r=["dd"], w=["ee"])
            self.ts("dve", den, ee, 1.0, None, ALU.add, r=["ee"], w=["den"])
            self.p.op("dve", lambda e: e.reciprocal(out=rr, in_=den), r=["den"], w=["rr"])
            self.tt("dve", self.wts[:, t, 0:1], rr, gw, ALU.mult, r=["rr", "gw"], w=[("wts", t)])
            self.tt("dve", self.wts[:, t, 1:2], self.wts[:, t, 0:1], ee, ALU.mult, r=[("wts", t), "ee"], w=[("wts", t)])
            pb = self.ps[1]
            self.mm(pb[:, 0:32], self.tstrict, mt, True, False, r=["mt"], w=[("ps", 1)])
            self.mm(pb[:, 0:32], self.ones, macc, False, True, r=["macc"], w=[("ps", 1)])
            self.tt("dve", posf, pb[:, 0:32], self.eoff, ALU.add, r=[("ps", 1)], w=["posf"])
            self.tt("dve", macc, macc, mt, ALU.add, r=["mt", "macc", ("ps", 1)], w=["macc"])
            for kk, ohk in enumerate((oh1, oh2)):
                self.tt("dve", tmp, ohk, posf, ALU.mult, r=["oh1", "oh2", "posf"], w=["tmp"])
                self.p.op("dve", lambda e, kk=kk: e.tensor_reduce(out=sf[:, kk:kk + 1], in_=tmp, axis=AX.X, op=ALU.add),
                          r=["tmp"], w=["sf"])
            self.ts("dve", sf, sf, 0.0, float(NSLOT + 127), ALU.max, r=["sf"], w=["sf"], op1=ALU.min)
            self.cp("dve", self.slots[:, t, :], sf, r=["sf"], w=[("slots", t)])
            for kk in range(2):
                self.p.dma("pool", lambda e, s=s, t=t, kk=kk: e.indirect_dma_start(
                    out=self.xg, out_offset=bass.IndirectOffsetOnAxis(ap=self.slots[:, t, kk:kk + 1], axis=0),
                    in_=x1b[s], in_offset=None), r=[k("x1b"), ("slots", t)], w=[("xg", t, kk)])
        self.p.barrier()
        ar.release(self.m_keep)

    def experts(self, l):
        ar = self.ar
        m0 = ar.mark()
        wg = [ar.alloc([DC, 512], BF16) for _ in range(2)]
        wu = [ar.alloc([DC, 512], BF16) for _ in range(2)]
        wd = [ar.alloc([4, D], BF16) for _ in range(2)]
        xe = [ar.alloc([2, D], BF16) for _ in range(2)]
        xeT = [ar.alloc([DC, 256], BF16) for _ in range(2)]
        sg = [ar.alloc([256], F32) for _ in range(2)]
        hT = [ar.alloc([4, 256], BF16) for _ in range(2)]
        yo = [ar.alloc([D], F32) for _ in range(2)]
        nb = 0
        ny = 0
        def issue_loads(e2):
            s2 = e2 % 2
            self.dma("sp", xe[s2], self.xg[e2 * CAP:(e2 + 1) * CAP, :].rearrange("(a p) n -> p a n", p=128),
                     r=[], w=[("xe", s2)])
            self.load_w(wg[s2], self.moe_w_gate[l][e2], ("wg", s2))
            self.load_w(wu[s2], self.moe_w_up[l][e2], ("wu", s2))
            self.dma("pool", wd[s2], self.moe_w_down[l][e2].rearrange("(c p) n -> p c n", p=128), r=[], w=[("wd", s2)])
        issue_loads(0)
        for e_ in range(NE):
            s = e_ % 2
            k = lambda n: (n, s)
            if e_ + 1 < NE:
                issue_loads(e_ + 1)
            for c4 in range(4):
                b = c4 % 2
                bank = self.psb(b).rearrange("p (a b) -> p a b", b=256)
                for cc in range(4):
                    c = c4 * 4 + cc
                    for rt in range(2):
                        self.tr(bank[:, cc, rt * 128:(rt + 1) * 128], xe[s][:, rt, c * 128:(c + 1) * 128], self.identb,
                                r=[k("xe")], w=[("ps", b)])
                self.cp("act" if c4 % 2 else "dve", xeT[s][:, c4 * 4:(c4 + 1) * 4, :], bank,
                        r=[("ps", b)], w=[k("xeT")])
            for fc in range(4):
                b = 2 + nb % 2
                nb += 1
                bank = self.ps[b]
                for c in range(DC):
                    self.mm(bank[:, 0:256], wg[s][:, c, fc * 128:(fc + 1) * 128], xeT[s][:, c, :], c == 0, c == DC - 1,
                            r=[k("wg"), k("xeT")], w=[("ps", b)])
                for c in range(DC):
                    self.mm(bank[:, 256:512], wu[s][:, c, fc * 128:(fc + 1) * 128], xeT[s][:, c, :], c == 0, c == DC - 1,
                            r=[k("wu"), k("xeT")], w=[("ps", b)])
                ss = nb % 2
                self.act(sg[ss], bank[:, 0:256], AF.Silu, r=[("ps", b)], w=[("sg", ss)])
                self.tt("dve", hT[s][:, fc, :], sg[ss], bank[:, 256:512], ALU.mult, r=[("sg", ss), ("ps", b)],
                        w=[k("hT")])
            for rt in range(2):
                ys = ny % 2
                ny += 1
                for cg in range(4):
                    b = 4 + cg
                    bank = self.ps[b]
                    for fc in range(4):
                        self.mm(bank[:, :], hT[s][:, fc, rt * 128:(rt + 1) * 128], wd[s][:, fc, cg * 512:(cg + 1) * 512],
                                fc == 0, fc == 3, r=[k("hT"), k("wd")], w=[("ps", b)])
                    self.cp("act" if cg % 2 else "dve", yo[ys][:, cg * 512:(cg + 1) * 512], bank[:, :],
                            r=[("ps", b)], w=[("yo", ys, cg)])
                r0 = e_ * CAP + rt * 128
                self.dma("sp", self.yg[r0:r0 + 128, :], yo[ys], r=[("yo", ys, cg) for cg in range(4)], w=[("yg", e_, rt)])
        self.p.barrier()
        ar.release(m0)

    def combine_ln(self, l, dst):
        ar = self.ar
        m0 = ar.mark()
        g_bc = ar.alloc([D], F32)
        b_bc = ar.alloc([D], F32)
        self.dma("sp", g_bc, self.ln_ffn_g[l].partition_broadcast(128), r=[], w=["lnp"])
        self.dma("sp", b_bc, self.ln_ffn_b[l].partition_broadcast(128), r=[], w=["lnp"])
        y1 = [ar.alloc([D], F32) for _ in range(2)]
        y2 = [ar.alloc([D], F32) for _ in range(2)]
        xt = [ar.alloc([D], F32) for _ in range(2)]
        ot = [ar.alloc([D], F32) for _ in range(2)]
        stats = ar.alloc([4, 6], F32)
        mv = ar.alloc([2], F32)
        rstd = ar.alloc([1], F32)
        nmr = ar.alloc([1], F32)
        for t in range(NT):
            s = t % 2
            k = lambda n: (n, s)
            for kk, yb in enumerate((y1[s], y2[s])):
                self.p.dma("pool", lambda e, t=t, kk=kk, yb=yb: e.indirect_dma_start(
                    out=yb, out_offset=None, in_=self.yg,
                    in_offset=bass.IndirectOffsetOnAxis(ap=self.slots[:, t, kk:kk + 1], axis=0)),
                    r=[], w=[k("y%d" % kk)])
            self.dma("sp", xt[s], self.x1[t * 128:(t + 1) * 128, :], r=[], w=[k("xt")])
            self.ts("pool", xt[s], xt[s], ALPHA, None, ALU.mult, r=[k("xt")], w=[k("xt")])
            self.stt("dve", xt[s], y1[s], self.wts[:, t, 0:1], xt[s], ALU.mult, ALU.add, r=[k("y0"), k("xt")], w=[k("xt")])
            self.stt("dve", xt[s], y2[s], self.wts[:, t, 1:2], xt[s], ALU.mult, ALU.add, r=[k("y1"), k("xt")], w=[k("xt")])
            for cg in range(4):
                self.p.op("dve", lambda e, s=s, cg=cg: e.bn_stats(out=stats[:, cg, :], in_=xt[s][:, cg * 512:(cg + 1) * 512]),
                          r=[k("xt")], w=["Bstats"])
            self.layer_norm_tile(xt[s], stats, mv, rstd, nmr, g_bc, b_bc, ot[s], k("xt"), k("ot"), "B")
            self.dma("sp", dst[t * 128:(t + 1) * 128, :], ot[s], r=[k("ot")], w=[("dst", t)])
        self.p.barrier()
        ar.release(m0)


def make_consts(core):
    h = core % 2
    c = {}
    c["c_identb"] = np.eye(128, dtype=np.float32).astype(ml_dtypes.bfloat16)
    c["c_identf"] = np.eye(128, dtype=np.float32)
    i = np.arange(128)
    c["c_trile"] = (i[:, None] <= i[None, :]).astype(np.float32).astype(ml_dtypes.bfloat16)
    c["c_btri"] = ((i[:, None] <= i[None, :]) & ((i[:, None] // 64) == (i[None, :] // 64))).astype(np.float32).astype(
        ml_dtypes.bfloat16)
    c["c_tstrict"] = (i[:, None] < i[None, :]).astype(np.float32)
    c["c_ones"] = np.ones((128, 128), np.float32)
    c["c_eoff"] = np.broadcast_to((np.arange(32) * CAP).astype(np.float32), (128, 32)).copy()
    gb = np.zeros((16, 16), np.float32)
    for qt in range(16):
        for p in range(16):
            valid = (h == 1) if p < 8 else ((p - 8) < qt // 2)
            gb[qt, p] = 0.0 if valid else NEG
    c["c_gbias"] = np.broadcast_to(gb.reshape(1, 256), (128, 256)).copy()
    c["c_hflag"] = np.full((128, 1), float(h), np.float32)
    pm = np.zeros((128, 2), np.float32)
    pm[:64, 0] = 1.0
    pm[64:, 1] = 1.0
    c["c_pmask"] = pm
    return c


_CACHE = {}


def kernel(**inputs):
    if "nc" not in _CACHE:
        _CACHE["nc"] = K().build()
    nc = _CACHE["nc"]
    x = np.asarray(inputs["x"], np.float32)
    in_maps = []
    BIG = ("moba_w_in", "moba_w_o", "hgrn_w_in", "hgrn_w_o", "moe_w_gate", "moe_w_up", "moe_w_down")
    for c in range(8):
        b, h = c // 2, c % 2
        m = {}
        for k, v in inputs.items():
            if k == "x":
                continue
            v = np.asarray(v, np.float32)
            if k in BIG and WMODE == "fake":
                continue
            if k in BIG and WMODE == "repl":
                m[k] = np.ascontiguousarray(v.reshape(-1, v.shape[-1]))
                continue
            if k in BIG:
                pieces = v.shape[0]
                C = v.shape[-1]
                v2 = v.reshape(pieces, 8, -1, C)[:, c]
                m[k] = np.ascontiguousarray(v2.reshape(-1, C))
            else:
                m[k] = np.ascontiguousarray(v)
        m["moe_b_re"] = m["moe_b_re"].reshape(DEPTH, 32)
        m["x"] = np.ascontiguousarray(x[b, h * T:(h + 1) * T, :])
        m.update(make_consts(c))
        in_maps.append(m)
    res = run_bass_kernel_spmd(nc, in_maps, core_ids=list(range(8)))
    out = np.zeros((4, 4096, D), np.float32)
    for c in range(8):
        b, h = c // 2, c % 2
        out[b, h * T:(h + 1) * T, :] = res.results[c]["out"]
    return out
```
